# Optimizing a Trainium2 kernel written in Bass

```python
import math
import jax, jax.numpy as jnp
from jax import lax
import numpy as np

D_MODEL = 2048
BATCH = 4
SEQ = 4096
DEPTH = 2

EPS = 1e-6
F32 = jnp.float32
N_GROUPS = 4
GROUP_WIDTH = D_MODEL // N_GROUPS
CHUNK = 64
GLA_HEADS = 4
GLA_DV = GROUP_WIDTH // GLA_HEADS
GLA_DK = GLA_DV // 2
GLA_GATE_RANK = 16
GLA_GATE_NORMALIZER = 16.0
RET_HEADS = 4
RET_DK = GROUP_WIDTH // RET_HEADS
RET_DV = GROUP_WIDTH // RET_HEADS
RET_ROT_BASE = 10000.0
SWA_Q_HEADS = 8
SWA_KV_HEADS = 2
SWA_HD = GROUP_WIDTH // SWA_Q_HEADS
WINDOW = 128
ROPE_THETA = 500000.0
ROPE_DIM = SWA_HD // 4
HG_HEADS = 4
HG_EXPAND = GROUP_WIDTH // HG_HEADS
HG_DV = GROUP_WIDTH // HG_HEADS
D_FF = ((8 * D_MODEL + 3 * 256 - 1) // (3 * 256)) * 256
IN_SPLITS = (
    GLA_HEADS * GLA_DK, GLA_HEADS * GLA_DK, GROUP_WIDTH, GROUP_WIDTH, GLA_GATE_RANK,
    RET_HEADS * RET_DK, RET_HEADS * RET_DK, GROUP_WIDTH, GROUP_WIDTH,
    SWA_Q_HEADS * SWA_HD, SWA_KV_HEADS * SWA_HD, SWA_KV_HEADS * SWA_HD,
    HG_HEADS * HG_EXPAND, HG_HEADS * HG_EXPAND, GROUP_WIDTH, GROUP_WIDTH,
)
IN_WIDTH = sum(IN_SPLITS)

kernel_name = "hymba_style_gla_ret_swa_hgrn2_adaln"


def rms_norm(x, w):
    xf = x.astype(F32)
    y = xf * lax.rsqrt(jnp.mean(xf * xf, axis=-1, keepdims=True) + EPS)
    return (y * w.astype(F32)).astype(x.dtype)


def head_rms_norm(o, w=None):
    of = o.astype(F32)
    y = of * lax.rsqrt(jnp.mean(of * of, axis=-1, keepdims=True) + EPS)
    if w is not None:
        y = y * w.astype(F32)
    return y.astype(o.dtype)


def rotate(x, positions, inv_freq):
    half = inv_freq.shape[0]
    ang = positions.astype(F32)[:, :, None] * inv_freq[None, None, :]
    cos = jnp.cos(ang)[:, :, None, :]
    sin = jnp.sin(ang)[:, :, None, :]
    xf = x.astype(F32)
    x1 = xf[..., :half]
    x2 = xf[..., half:2 * half]
    out = jnp.concatenate([x1 * cos - x2 * sin, x2 * cos + x1 * sin, xf[..., 2 * half:]], axis=-1)
    return out.astype(x.dtype)


def gated_chunk_scan(q, k, v, log_g):
    B, S, H, dk = q.shape
    dv = v.shape[-1]
    n = S // CHUNK

    def to_chunks(t):
        return t.astype(F32).reshape(B, n, CHUNK, H, t.shape[-1]).transpose(1, 0, 3, 2, 4)

    qc, kc, vc, gc = to_chunks(q), to_chunks(k), to_chunks(v), to_chunks(log_g)
    causal = jnp.tril(jnp.ones((CHUNK, CHUNK), dtype=bool))[:, :, None]

    def step(state, inp):
        qi, ki, vi, gi = inp
        b = jnp.cumsum(gi, axis=-2)
        diff = b[:, :, :, None, :] - b[:, :, None, :, :]
        decay = jnp.exp(jnp.where(causal, diff, -jnp.inf))
        attn = jnp.einsum('bhik,bhjk,bhijk->bhij', qi, ki, decay)
        o = (jnp.einsum('bhij,bhjv->bhiv', attn, vi)
             + jnp.einsum('bhik,bhkv->bhiv', qi * jnp.exp(b), state))
        b_last = b[:, :, -1, :]
        k_dec = ki * jnp.exp(b_last[:, :, None, :] - b)
        state = state * jnp.exp(b_last)[..., None] + jnp.einsum('bhjk,bhjv->bhkv', k_dec, vi)
        return state, o

    _, o = lax.scan(step, jnp.zeros((B, H, dk, dv), F32), (qc, kc, vc, gc))
    return o.transpose(1, 0, 3, 2, 4).reshape(B, S, H, dv).astype(v.dtype)


def retention_chunkwise(q, k, v, log_gamma):
    B, S, H, dk = q.shape
    dv = v.shape[-1]
    n = S // CHUNK

    def to_chunks(t):
        return t.astype(F32).reshape(B, n, CHUNK, H, t.shape[-1]).transpose(0, 3, 1, 2, 4)

    qc, kc, vc = to_chunks(q), to_chunks(k), to_chunks(v)
    pos = jnp.arange(CHUNK, dtype=F32)
    rel = pos[:, None] - pos[None, :]
    dmask = jnp.exp(jnp.where(rel >= 0, log_gamma[:, None, None] * rel, -jnp.inf))
    scores = jnp.einsum('bhnik,bhnjk->bhnij', qc, kc) * dmask[None, :, None]
    inner = jnp.einsum('bhnij,bhnjv->bhniv', scores, vc)
    zeta = jnp.exp(log_gamma[:, None] * (CHUNK - 1 - pos))
    chunk_kv = jnp.einsum('bhnjk,hj,bhnjv->nbhkv', kc, zeta, vc)
    chunk_decay = jnp.exp(log_gamma * CHUNK)[None, :, None, None]

    def step(r, kv):
        return r * chunk_decay + kv, r

    _, r_prev = lax.scan(step, jnp.zeros((B, H, dk, dv), F32), chunk_kv)
    xi = jnp.exp(log_gamma[:, None] * (pos + 1.0))
    cross = jnp.einsum('bhnik,hi,nbhkv->bhniv', qc, xi, r_prev)
    o = (inner + cross).transpose(0, 2, 3, 1, 4).reshape(B, S, H, dv)
    return o.astype(v.dtype)


def sliding_window_sink_attention(q, k, v, sinks):
    B, S, Hq, hd = q.shape
    Hkv = k.shape[2]
    G = Hq // Hkv
    nb = S // WINDOW
    qb = q.reshape(B, nb, WINDOW, Hkv, G, hd)
    kb = k.reshape(B, nb, WINDOW, Hkv, hd)
    vb = v.reshape(B, nb, WINDOW, Hkv, hd)

    def with_prev(t):
        prev = jnp.concatenate([jnp.zeros_like(t[:, :1]), t[:, :-1]], axis=1)
        return jnp.concatenate([prev, t], axis=2)

    kk, vv = with_prev(kb), with_prev(vb)
    s = jnp.einsum('bnqhgd,bnkhd->bnhgqk', qb, kk).astype(F32) * (hd ** -0.5)
    qpos = jnp.arange(WINDOW) + WINDOW
    kpos = jnp.arange(2 * WINDOW)
    rel = qpos[:, None] - kpos[None, :]
    band = (rel >= 0) & (rel < WINDOW)
    has_prev = (jnp.arange(nb)[:, None] > 0) | (kpos[None, :] >= WINDOW)
    mask = band[None] & has_prev[:, None, :]
    s = jnp.where(mask[None, :, None, None], s, -jnp.inf)
    sink = jnp.broadcast_to(sinks.astype(F32).reshape(1, 1, Hkv, G, 1, 1), s.shape[:-1] + (1,))
    p = jax.nn.softmax(jnp.concatenate([s, sink], axis=-1), axis=-1)[..., :-1]
    o = jnp.einsum('bnhgqk,bnkhd->bnqhgd', p.astype(v.dtype), vv)
    return o.reshape(B, S, Hq * hd)


def mixer_groups(h, positions, w_in, gla_w2, gla_b2, gla_nw, sinks, lb, hg_nw):
    B, S, _ = h.shape
    proj = h @ w_in
    offs = np.cumsum(IN_SPLITS)[:-1].tolist()
    (a_q, a_k, a_v, a_g, a_r,
     b_q, b_k, b_v, b_g,
     c_q, c_k, c_v,
     d_q, d_f, d_i, d_g) = jnp.split(proj, offs, axis=-1)

    def heads(t, n):
        return t.reshape(B, S, n, -1)

    a_log = jax.nn.log_sigmoid((a_r @ gla_w2 + gla_b2).astype(F32)) / GLA_GATE_NORMALIZER
    a_o = gated_chunk_scan(heads(a_q, GLA_HEADS) * (GLA_DK ** -0.5), heads(a_k, GLA_HEADS),
                           heads(a_v, GLA_HEADS), heads(a_log, GLA_HEADS))
    a_out = (head_rms_norm(a_o, gla_nw) * jax.nn.silu(heads(a_g, GLA_HEADS))).reshape(B, S, GROUP_WIDTH)

    ret_inv = 1.0 / jnp.power(RET_ROT_BASE, jnp.linspace(0.0, 1.0, RET_DK // 2, dtype=F32))
    log_gamma = jnp.log1p(-jnp.power(2.0, -5.0 - jnp.arange(RET_HEADS, dtype=F32)))
    rq = rotate(heads(b_q, RET_HEADS), positions, ret_inv)
    rk = rotate(heads(b_k, RET_HEADS), positions, ret_inv) * (RET_DK ** -0.5)
    b_o = retention_chunkwise(rq, rk, heads(b_v, RET_HEADS), log_gamma)
    b_out = (head_rms_norm(b_o) * jax.nn.silu(heads(b_g, RET_HEADS))).reshape(B, S, GROUP_WIDTH)

    rope_inv = 1.0 / jnp.power(ROPE_THETA, jnp.arange(ROPE_DIM // 2, dtype=F32) / (ROPE_DIM // 2))
    sq = rotate(heads(c_q, SWA_Q_HEADS), positions, rope_inv)
    sk = rotate(heads(c_k, SWA_KV_HEADS), positions, rope_inv)
    c_out = sliding_window_sink_attention(sq, sk, heads(c_v, SWA_KV_HEADS), sinks)

    lb_h = lb.astype(F32).reshape(HG_HEADS, HG_EXPAND)
    f = lb_h + (1.0 - lb_h) * jax.nn.sigmoid(heads(d_f, HG_HEADS).astype(F32))
    hq = jax.nn.silu(heads(d_q, HG_HEADS)) * (HG_EXPAND ** -0.5)
    d_o = gated_chunk_scan(hq, 1.0 - f, heads(d_i, HG_HEADS), jnp.log(f))
    d_out = (head_rms_norm(d_o, hg_nw) * jax.nn.silu(heads(d_g, HG_HEADS))).reshape(B, S, GROUP_WIDTH)

    return jnp.concatenate([a_out, b_out, c_out, d_out], axis=-1)


def setup_inputs(seed: int = 0) -> dict:
    key = jax.random.key(seed)
    ks = jax.random.split(key, 20)

    def nrm(k, shape, scale):
        return jax.random.normal(k, shape, F32) * scale

    return {
        "x": nrm(ks[0], (BATCH, SEQ, D_MODEL), 1.0),
        "c": nrm(ks[1], (BATCH, D_MODEL), 1.0),
        "positions": jnp.broadcast_to(jnp.arange(SEQ, dtype=jnp.int32), (BATCH, SEQ)),
        "w_ada": nrm(ks[2], (DEPTH, D_MODEL, 6 * D_MODEL), 0.5 * D_MODEL ** -0.5),
        "b_ada": nrm(ks[3], (DEPTH, 6 * D_MODEL), 0.02),
        "norm1_w": 1.0 + nrm(ks[4], (DEPTH, D_MODEL), 0.02),
        "w_in": nrm(ks[5], (DEPTH, D_MODEL, IN_WIDTH), D_MODEL ** -0.5),
        "gla_gate_w2": nrm(ks[6], (DEPTH, GLA_GATE_RANK, GLA_HEADS * GLA_DK), GLA_GATE_RANK ** -0.5),
        "gla_gate_b2": nrm(ks[7], (DEPTH, GLA_HEADS * GLA_DK), 0.1),
        "gla_norm_w": 1.0 + nrm(ks[8], (DEPTH, GLA_DV), 0.02),
        "swa_sinks": nrm(ks[9], (DEPTH, SWA_Q_HEADS), 0.5),
        "hgrn_lb": nrm(ks[10], (DEPTH, HG_HEADS * HG_EXPAND), 1.0),
        "hgrn_norm_w": 1.0 + nrm(ks[11], (DEPTH, HG_DV), 0.02),
        "w_out": nrm(ks[12], (DEPTH, D_MODEL, D_MODEL), D_MODEL ** -0.5),
        "norm2_w": 1.0 + nrm(ks[13], (DEPTH, D_MODEL), 0.02),
        "w_ffn_in": nrm(ks[14], (DEPTH, D_MODEL, 2 * D_FF), D_MODEL ** -0.5),
        "w_ffn_down": nrm(ks[15], (DEPTH, D_FF, D_MODEL), D_FF ** -0.5),
        "final_norm_w": 1.0 + nrm(ks[16], (D_MODEL,), 0.02),
    }


def reference(x, c, positions, w_ada, b_ada, norm1_w, w_in, gla_gate_w2, gla_gate_b2,
              gla_norm_w, swa_sinks, hgrn_lb, hgrn_norm_w, w_out, norm2_w,
              w_ffn_in, w_ffn_down, final_norm_w):
    lb_soft = jax.nn.softmax(hgrn_lb.astype(F32), axis=0)
    lower_bounds = jnp.cumsum(lb_soft, axis=0) - lb_soft[0]
    cond = jax.nn.silu(c)
    for l in range(DEPTH):
        mod = cond @ w_ada[l] + b_ada[l]
        shift1, scale1, gate1, shift2, scale2, gate2 = [m[:, None, :] for m in jnp.split(mod, 6, axis=-1)]
        h = rms_norm(x, norm1_w[l]) * (1.0 + scale1) + shift1
        mixed = mixer_groups(h, positions, w_in[l], gla_gate_w2[l], gla_gate_b2[l], gla_norm_w[l],
                             swa_sinks[l], lower_bounds[l], hgrn_norm_w[l])
        x = x + gate1 * (mixed @ w_out[l])
        h = rms_norm(x, norm2_w[l]) * (1.0 + scale2) + shift2
        gate_up = h @ w_ffn_in[l]
        g_ff, u_ff = jnp.split(gate_up, 2, axis=-1)
        x = x + gate2 * ((jax.nn.silu(g_ff) * u_ff) @ w_ffn_down[l])
    return rms_norm(x, final_norm_w)
```

```python
import math
from contextlib import ExitStack
import numpy as np
import concourse.bass as bass
import concourse.mybir as mybir
from concourse.bass_utils import run_bass_kernel_spmd

F32 = mybir.dt.float32
BF16 = mybir.dt.bfloat16
I32 = mybir.dt.int32
AF = mybir.ActivationFunctionType
ALU = mybir.AluOpType
AX = mybir.AxisListType

SAME_ENGINE_RAW = True

D = 2048
KC = 16
SEQ = 4096
TT = 512
DFF = 5632
FC = 44
INW = 6416
EPS = 1e-6
NCORES = 4
TWO_PI = 2.0 * math.pi

C_IDENT = 0
C_ONES = 128
C_MASKA = 256
C_PERM_RET = 384
C_PERM_SWQ = 512
C_DUP = 640
C_SWAMASK = 1152
C_SWAMASK0 = 1408
C_RET_EQ = 1664
C_RET_EK = 1920
C_COLS = 2176
NCON = 2192

V_LAYER = 136
V_BADA, V_N1, V_N2, V_B2, V_LB, V_GNW, V_HNW = 0, 96, 112, 128, 130, 134, 135
V_FN = 272
V_C = 288
V_ROWS = 384


class Op:
    __slots__ = ("eng", "fn", "deps", "is_dma", "sem", "val", "signal", "epoch")

    def __init__(self, eng, fn, is_dma, epoch):
        self.eng = eng
        self.fn = fn
        self.is_dma = is_dma
        self.deps = []
        self.sem = None
        self.val = 0
        self.signal = is_dma
        self.epoch = epoch


class SemRing:
    def __init__(self, sems):
        self.sems = list(sems)
        self.cnt = [0] * len(self.sems)
        self.last = [None] * len(self.sems)
        self.i = 0


class Sched:
    ENGS = ("pe", "act", "dve", "pool", "sp")

    def __init__(self):
        self.ops = []
        self.last_w = {}
        self.readers = {}
        self.epoch = 0

    def _deps(self, op, r, w):
        deps = {}

        def add(y, kind):
            if y is None or y is op:
                return
            k = deps.get(id(y))
            if k is None:
                deps[id(y)] = [y, {kind}]
            else:
                k[1].add(kind)

        for k in r:
            add(self.last_w.get(k), "raw")
        for k in w:
            add(self.last_w.get(k), "waw")
            for rd in self.readers.get(k, ()):
                add(rd, "war")
        for y, kinds in deps.values():
            keep = False
            if y.is_dma or op.is_dma:
                keep = True
            elif y.eng != op.eng:
                keep = True
            elif op.eng != "pe" and SAME_ENGINE_RAW and "raw" in kinds:
                keep = True
            if keep:
                op.deps.append(y)
                y.signal = True
        for k in r:
            self.readers.setdefault(k, []).append(op)
        for k in w:
            self.last_w[k] = op
            self.readers[k] = []

    def op(self, eng, fn, r=(), w=()):
        o = Op(eng, fn, False, self.epoch)
        self._deps(o, r, w)
        self.ops.append(o)
        return o

    def dma(self, eng, ring, out, in_, r=(), w=()):
        o = Op(eng, (lambda e, out=out, in_=in_: e.dma_start(out=out, in_=in_)), True, self.epoch)
        self._deps(o, r, w)
        i = ring.i
        ring.i = (ring.i + 1) % len(ring.sems)
        prev = ring.last[i]
        if prev is not None and prev not in o.deps:
            o.deps.append(prev)
        ring.cnt[i] += 16
        o.sem = ring.sems[i]
        o.val = ring.cnt[i]
        ring.last[i] = o
        self.ops.append(o)
        return o

    def emit(self, block, eng_sems):
        cnt = {}
        for o in self.ops:
            if not o.is_dma and o.signal:
                key = (o.eng, o.epoch)
                cnt[key] = cnt.get(key, 0) + 1
                o.sem = eng_sems[o.eng][o.epoch]
                o.val = cnt[key]
        per = {e: [] for e in self.ENGS}
        for o in self.ops:
            per[o.eng].append(o)
        self.stats = {e: len(per[e]) for e in self.ENGS}
        self.maxcnt = max(cnt.values()) if cnt else 0

        def run(engname, e):
            waited = {}
            for o in per[engname]:
                need = {}
                for y in o.deps:
                    key = id(y.sem)
                    if key not in need or need[key][1] < y.val:
                        need[key] = (y.sem, y.val)
                for key, (sem, val) in need.items():
                    if waited.get(key, 0) >= val:
                        continue
                    e.wait_ge(sem, val)
                    waited[key] = val
                if o.fn is None:
                    continue
                ins = o.fn(e)
                if o.signal:
                    ins.then_inc(o.sem, 16 if o.is_dma else 1)

        @block.tensor
        def _(e):
            run("pe", e)

        @block.scalar
        def _(e):
            run("act", e)

        @block.vector
        def _(e):
            run("dve", e)

        @block.gpsimd
        def _(e):
            run("pool", e)

        @block.sync
        def _(e):
            run("sp", e)


def make_consts():
    c = np.zeros((128, NCON), np.float64)
    c[:, C_IDENT:C_IDENT + 128] = np.eye(128)
    c[:, C_ONES:C_ONES + 128] = 1.0
    j = np.arange(128)[:, None]
    i = np.arange(128)[None, :]
    c[:, C_MASKA:C_MASKA + 128] = ((j // 64 == i // 64) & (j <= i)).astype(np.float64)
    k = np.arange(128)[:, None]
    m = np.arange(128)[None, :]
    c[:, C_PERM_RET:C_PERM_RET + 128] = (k == (m + 64) % 128)

    def perm64(d):
        return np.where(d < 8, d + 8, np.where(d < 16, d - 8, d))

    c[:, C_PERM_SWQ:C_PERM_SWQ + 128] = (k == 64 * (m // 64) + perm64(m % 64))
    for kvh in range(2):
        c[:, C_DUP + (2 * kvh) * 128:C_DUP + (2 * kvh + 1) * 128] = (k == 64 * kvh + (m % 64))
        c[:, C_DUP + (2 * kvh + 1) * 128:C_DUP + (2 * kvh + 2) * 128] = (k == 64 * kvh + perm64(m % 64))
    q = np.arange(128)[:, None] + 128
    kp = np.arange(256)[None, :]
    rel = q - kp
    band = (rel >= 0) & (rel < 128)
    c[:, C_SWAMASK:C_SWAMASK + 256] = np.where(band, 0.0, -30000.0)
    c[:, C_SWAMASK0:C_SWAMASK0 + 256] = np.where(band & (kp >= 128), 0.0, -30000.0)
    jj = np.arange(64)
    for h in range(4):
        lg = math.log1p(-(2.0 ** (-5.0 - h)))
        c[:, C_RET_EQ + h * 64:C_RET_EQ + (h + 1) * 64] = np.exp(lg * (jj - 31))[None, :]
        c[:, C_RET_EK + h * 64:C_RET_EK + (h + 1) * 64] = np.exp(lg * (31 - jj))[None, :]
        c[:, C_COLS + 4 + 3 * h + 0] = math.exp(lg * 32)
        c[:, C_COLS + 4 + 3 * h + 1] = math.exp(lg * 64)
        c[:, C_COLS + 4 + 3 * h + 2] = math.exp(lg * 32)
    ret_inv = (1.0 / np.power(np.float32(10000.0), np.linspace(0.0, 1.0, 64, dtype=np.float32))).astype(np.float32)
    p = np.arange(128)
    c[:, C_COLS + 0] = ret_inv[p % 64]
    c[:, C_COLS + 1] = np.where(p < 64, -1.0, 1.0)
    rope_inv = (1.0 / np.power(np.float32(500000.0), np.arange(8, dtype=np.float32) / np.float32(8.0))).astype(np.float32)
    d = p % 64
    c[:, C_COLS + 2] = np.where(d < 16, rope_inv[d % 8], 0.0)
    c[:, C_COLS + 3] = np.where(d < 8, -1.0, np.where(d < 16, 1.0, 0.0))
    return c.astype(np.float32)


class _Stop(Exception):
    pass


def build(NT=8, NL=2, taps=None, stop=None):
    taps = taps or []

    def chk(name):
        if stop == name:
            raise _Stop()
    nc = bass.Bass("TRN2", target_bir_lowering=False)
    dram = lambda n, s, d, k="ExternalInput": nc.dram_tensor(n, s, d, kind=k).ap()
    x_d = dram("x", [SEQ, D], F32)
    pos_d = dram("pos", [1, SEQ], I32)
    vecs_d = dram("vecs", [V_ROWS, 128], F32)
    consts_d = dram("consts", [128, NCON], F32)
    wada_d = dram("w_ada", [2, D, 6 * D], F32)
    win_d = dram("w_in", [2, D, INW], F32)
    w2_d = dram("w2", [2, 16, 256], F32)
    sinks_d = dram("sinks", [1, 16], F32)
    wout_d = dram("w_out", [2, D, D], F32)
    wffi_d = dram("w_ffn_in", [2, D, 2 * DFF], F32)
    wffd_d = dram("w_ffn_down", [2, DFF, D], F32)
    out_d = dram("out", [SEQ, D], F32, "ExternalOutput")
    NSCR = 64
    wsc = nc.dram_tensor("wsc", [2, NSCR, 128, 8192], BF16).ap()
    tap_d = {}

    S = Sched()
    es = ExitStack()
    sb = lambda n, s, d: es.enter_context(nc.sbuf_tensor("s_" + n, s, d))
    mksem = lambda n: es.enter_context(nc.semaphore(n))

    consts = sb("consts", [128, NCON], F32)
    vstage = sb("vstage", [128, 3, 128], F32)
    vT = sb("vT", [128, V_ROWS], F32)
    identb = sb("identb", [128, 128], BF16)
    onesb = sb("onesb", [128, 128], BF16)
    condb = sb("condb", [128, 16], BF16)
    modT = sb("modT", [128, 2, 96], F32)
    nrm = sb("nrm", [128, 2, 2, 16], F32)
    smallc = sb("smallc", [128, 64], F32)
    sinkb = sb("sinkb", [128, 16], F32)
    w2b = sb("w2b", [16, 2, 256], BF16)
    xT = sb("xT", [128, KC, TT], F32)
    hT = sb("hT", [128, KC, TT], BF16)
    mixT = sb("mixT", [128, KC, TT], BF16)
    NSLAB = 2
    SLAB_E = 8192
    slab = sb("slab", [128, NSLAB, SLAB_E], BF16)
    stage = sb("stage", [128, 2, D], F32)
    rot = sb("rot", [128, 4, TT], F32)
    msk = sb("msk", [128, TT], F32)
    ST = sb("ST", [128, 2, 12, 128], F32)
    SKD = sb("SKD", [128, 2, 2, 640], BF16)
    SV = sb("SV", [128, 2, 5, 128], BF16)
    junk = sb("junk", [128, 8], F32)
    ARENA_B = 50688
    arena = sb("arena", [128, ARENA_B // 2], BF16)

    def carve(off, nbytes, dt):
        v = arena[:, off // 2:(off + nbytes) // 2]
        return v if dt == BF16 else v.bitcast(dt)

    EQ = carve(0, 8192, F32).rearrange("p (h t) -> p h t", t=TT)
    QT = carve(8192, 4096, BF16).rearrange("p (h t) -> p h t", t=TT)
    KT = carve(12288, 4096, BF16).rearrange("p (h t) -> p h t", t=TT)
    VTM = carve(16384, 4096, BF16).rearrange("p (g c) -> p g c", c=512)
    GATE = carve(20480, 4096, BF16).rearrange("p (h t) -> p h t", t=TT)
    T = [carve(24576 + 2048 * i, 2048, F32) for i in range(5)]
    rti = T[2].bitcast(I32)
    posi = T[3].bitcast(I32)
    posf = T[4]
    T12 = carve(24576, 4096, F32)
    T34b = carve(24576 + 4096, 4096, BF16)
    KTT = [carve(34816 + 1024 * i, 1024, BF16).rearrange("p (g c) -> p g c", c=128) for i in range(2)]
    ATb = [carve(36864 + 1024 * i, 1024, BF16) for i in range(2)]
    SBF = [carve(38912 + 2048 * i, 2048, BF16).rearrange("p (c v) -> p c v", v=128) for i in range(2)]
    XS = [carve(43008 + 512 * i, 512, F32) for i in range(2)]
    SQb = carve(44032, 2048, F32)
    RSTD = carve(46080, 2048, F32)
    TTb = carve(48128, 2048, F32)
    EC = carve(50176, 384, F32).rearrange("p (k h c) -> p k h c", k=3, h=4)
    HID = carve(0, 45056, BF16).rearrange("p (j t) -> p j t", t=TT)
    FS = [carve(45056 + 2048 * i, 2048, F32) for i in range(2)]
    EQK = [("EQ", 0), ("EQ", 1), ("EQ", 2)]
    arena_keys = (EQK + ["QT", "KT", "VTM", "GATE"] + [("T", i) for i in range(5)] + [("KTT", i) for i in range(2)]
                  + [("AT", i) for i in range(2)] + [("SBF", i) for i in range(2)] + [("XS", i) for i in range(2)]
                  + ["SQb", "RSTD", "TTb", "EC", ("FS", 0), ("FS", 1)] + [("HID", k) for k in range(FC)])

    pb = [es.enter_context(nc.psum_tensor("pb%d" % i, [128, 512], F32)) for i in range(8)]
    pbk = lambda i: ("pb", i)

    ringS = SemRing([mksem("rs%d" % i) for i in range(4)])
    ringW = SemRing([mksem("rw%d" % i) for i in range(4)])
    ringB = SemRing([mksem("rb%d" % i) for i in range(2)])
    NEPOCH = NT + 1
    esems = {e: [mksem("e_%s_%d" % (e, k)) for k in range(NEPOCH)] for e in ("pe", "act", "dve", "pool")}
    esems["sp"] = [mksem("e_sp")] * NEPOCH

    cc = lambda c0, n=128: consts[:, c0:c0 + n]
    ccol = lambda j: consts[:, C_COLS + j:C_COLS + j + 1]
    identf = cc(C_IDENT)
    onesf = cc(C_ONES)

    rr = {"proj": 0, "cp": 0, "slab": 0, "scratch": False, "layer": 0, "tt": 0, "sidx": -1}

    def pbank():
        b = rr["proj"]
        rr["proj"] = (b + 1) % 3
        return b

    def cpeng():
        rr["cp"] ^= 1
        return "act" if rr["cp"] else "dve"

    def copy_op(eng, out, in_, r, w):
        if eng == "act":
            S.op("act", lambda e: e.activation(out=out, in_=in_, func=AF.Copy), r=r, w=w)
        else:
            S.op(eng, lambda e: e.tensor_copy(out=out, in_=in_), r=r, w=w)

    def load_slab(w2d, c0, ncols, kcn, half=None):
        first = (half is None or half == 0)
        if first:
            s = rr["slab"]
            rr["slab"] = (s + 1) % NSLAB
            rr["cur"] = s
            rr["sidx"] += 1
        s = rr["cur"]
        idx, lyr = rr["sidx"], rr["layer"]
        off = 0 if not half else half * (SLAB_E // 2)
        view = slab[:, s, off:off + kcn * ncols].rearrange("p (k n) -> p k n", n=ncols)
        both = [("slab", s, 0), ("slab", s, 1)]
        wk = [("slab", s, half)] if half is not None else both
        if not rr["scratch"] or rr["tt"] <= lyr:
            src = w2d[:, c0:c0 + ncols].rearrange("(k p) n -> p k n", p=128)
            S.dma("pool", ringW, view, src, w=wk)
            if rr["scratch"] and rr["tt"] == lyr and (half is None or half == 1):
                S.dma("sp", ringB, wsc[lyr, idx], slab[:, s, :], r=both, w=[("wsc", lyr, idx)])
        elif first:
            n = SLAB_E if half is not None else kcn * ncols
            S.dma("pool", ringW, slab[:, s, 0:n], wsc[lyr, idx][:, 0:n], r=[("wsc", lyr, idx)], w=both)
        return view, wk

    def proj_fm(sl, slk, j0, M, bank, rhsT, rkeys, kcn=KC):
        def fn(e):
            ins = None
            for kc in range(kcn):
                ins = e.matmul(pb[bank][0:M, :], lhsT=sl[:, kc, j0:j0 + M], rhs=rhsT[:, kc, :],
                               start=(kc == 0), stop=(kc == kcn - 1))
            return ins
        S.op("pe", fn, r=list(slk) + list(rkeys), w=[pbk(bank)])

    def proj_tm(sl, slk, n0, ncols, tg, bank):
        def fn(e):
            ins = None
            for kc in range(KC):
                ins = e.matmul(pb[bank][:, 0:ncols], lhsT=hT[:, kc, tg * 128:(tg + 1) * 128],
                               rhs=sl[:, kc, n0:n0 + ncols], start=(kc == 0), stop=(kc == KC - 1))
            return ins
        S.op("pe", fn, r=list(slk) + ["hT"], w=[pbk(bank)])

    def tap(name, ap, shape, keys):
        if name not in taps:
            return
        t = dram("tap_" + name, list(shape), F32, "ExternalOutput")
        tap_d[name] = t
        if ap.dtype == F32:
            S.dma("sp", ringS, t, ap, r=keys, w=[("tap", name)])
        else:
            S.dma("pool", ringW, t, ap, r=keys, w=[("tap", name)])

    ssq_pending = []

    def ssq_flush():
        while ssq_pending:
            ssq_pending.pop(0)()

    def barrier():
        ssq_flush()
        S.op("dve", lambda e: e.memset(junk[:, 0:1], 0.0), w=arena_keys)

    S.dma("sp", ringS, consts[:], consts_d, w=["consts"])
    S.dma("sp", ringS, vstage[:], vecs_d.rearrange("(g r) c -> r g c", g=3), w=["vstage"])
    S.dma("sp", ringS, sinkb[:], bass.AP(sinks_d.tensor, 0, [[0, 128], [1, 16]]), w=["sinkb"])
    S.dma("pool", ringW, w2b[:], w2_d.rearrange("l k n -> k l n"), w=["w2b"])

    def fn(e):
        ins = None
        for g in range(3):
            ins = e.transpose(out=pb[0][:, g * 128:(g + 1) * 128], in_=vstage[:, g, :], identity=identf)
        return ins
    S.op("pe", fn, r=["consts", "vstage"], w=[pbk(0)])
    S.op("dve", lambda e: e.tensor_copy(out=vT[:], in_=pb[0][:, 0:V_ROWS]), r=[pbk(0)], w=["vT"])
    S.op("dve", lambda e: e.tensor_copy(out=identb[:], in_=identf), r=["consts"], w=["identb"])
    S.op("dve", lambda e: e.tensor_copy(out=onesb[:], in_=onesf), r=["consts"], w=["onesb"])
    S.op("act", lambda e: e.activation(out=condb[:], in_=vT[:, V_C:V_C + 16], func=AF.Silu), r=["vT"], w=["condb"])
    S.op("pool", lambda e: e.memset(msk[:], 1.0), w=["msk"])
    S.op("pool", lambda e: e.memset(msk[:].rearrange("p (c j) -> p c j", j=64)[:, :, 0:1], 0.0), r=["msk"], w=["msk"])
    S.op("pool", lambda e: e.memset(ST[:], 0.0), w=["ST"])
    S.op("pool", lambda e: e.memset(SKD[:], 0.0), w=["SKD"])
    S.op("pool", lambda e: e.memset(SV[:], 0.0), w=["SV"])
    LB0, OML0, NB2 = 0, 8, 16
    S.op("dve", lambda e: e.memset(smallc[:, 0:4], 0.0), w=["smallc"])
    S.op("dve", lambda e: e.tensor_tensor(out=smallc[:, 32:36], in0=vT[:, V_LB:V_LB + 4],
                                          in1=vT[:, V_LAYER + V_LB:V_LAYER + V_LB + 4], op=ALU.subtract), r=["vT"], w=["smallc_t"])
    S.op("act", lambda e: e.activation(out=smallc[:, 36:40], in_=smallc[:, 32:36], func=AF.Exp), r=["smallc_t"], w=["smallc_t2"])
    S.op("dve", lambda e: e.tensor_scalar(out=smallc[:, 40:44], in0=smallc[:, 36:40], scalar1=1.0, scalar2=None, op0=ALU.add),
         r=["smallc_t2"], w=["smallc_t3"])
    S.op("dve", lambda e: e.reciprocal(out=smallc[:, 4:8], in_=smallc[:, 40:44]), r=["smallc_t3", "smallc"], w=["smallc"])
    S.op("dve", lambda e: e.tensor_scalar(out=smallc[:, 8:16], in0=smallc[:, 0:8], scalar1=-1.0, scalar2=1.0,
                                          op0=ALU.mult, op1=ALU.add), r=["smallc"], w=["smallc"])
    for l in range(2):
        S.op("dve", lambda e, l=l: e.tensor_scalar(out=smallc[:, NB2 + 2 * l:NB2 + 2 * l + 2],
                                                   in0=vT[:, l * V_LAYER + V_B2:l * V_LAYER + V_B2 + 2],
                                                   scalar1=-1.0, scalar2=None, op0=ALU.mult), r=["vT", "smallc"], w=["smallc"])

    for l in range(NL):
        for s in range(24):
            sl, slk = load_slab(wada_d[l], s * 512, 512, KC)

            def fn(e, sl=sl, s=s):
                ins = None
                for j in range(4):
                    col = s * 4 + j
                    for kc in range(KC):
                        ins = e.matmul(pb[3][:, col:col + 1], lhsT=sl[:, kc, j * 128:(j + 1) * 128],
                                       rhs=condb[:, kc:kc + 1], start=(kc == 0), stop=(kc == KC - 1))
                return ins
            S.op("pe", fn, r=slk + ["condb"], w=[pbk(3)])
        S.op("dve", lambda e, l=l: e.tensor_tensor(out=modT[:, l, :], in0=pb[3][:, 0:96],
                                                   in1=vT[:, l * V_LAYER:l * V_LAYER + 96], op=ALU.add),
             r=[pbk(3), "vT"], w=["modT"])
        for k, (nrow, sc0) in enumerate(((V_N1, 16), (V_N2, 64))):
            S.op("dve", lambda e, l=l, k=k, nrow=nrow, sc0=sc0: e.scalar_tensor_tensor(
                out=nrm[:, l, k, :], in0=modT[:, l, sc0:sc0 + 16], scalar=1.0,
                in1=vT[:, l * V_LAYER + nrow:l * V_LAYER + nrow + 16], op0=ALU.add, op1=ALU.mult),
                r=["modT", "vT"], w=["nrm"])
    tap("modT", modT[:].rearrange("p l j -> p (l j)"), [128, 192], ["modT"])

    def ssq_chunk(c, which, defer=False):
        if which == "T":
            sq, key = T[c % 2].bitcast(BF16)[:, 0:TT], ("T", c % 2)
        else:
            sq, key = FS[c % 2].bitcast(BF16)[:, 0:TT], ("FS", c % 2)
        prev = list(ssq_pending)
        del ssq_pending[:]
        if c % 2 == 0:
            S.op("act", lambda e: e.activation(out=sq, in_=xT[:, c, :], func=AF.Square), r=[("xT", c)], w=[key])
        else:
            S.op("dve", lambda e: e.tensor_tensor(out=sq, in0=xT[:, c, :], in1=xT[:, c, :], op=ALU.mult), r=[("xT", c)], w=[key])

        def mm():
            S.op("pe", lambda e: e.matmul(pb[7][:], lhsT=onesb[:], rhs=sq, start=(c == 0), stop=(c == KC - 1)),
                 r=[key, "onesb"], w=[pbk(7)])
        for p in prev:
            p()
        if defer:
            ssq_pending.append(mm)
        else:
            mm()

    def rstd_finish():
        ssq_flush()
        S.op("act", lambda e: e.activation(out=RSTD, in_=pb[7][:], func=AF.Ln, scale=1.0 / D, bias=EPS), r=[pbk(7)], w=["RSTD"])
        S.op("act", lambda e: e.activation(out=RSTD, in_=RSTD, func=AF.Exp, scale=-0.5), r=["RSTD"], w=["RSTD"])

    def ssq_rstd(have_ssq=False):
        if not have_ssq:
            for c in range(KC):
                ssq_chunk(c, "T")
        rstd_finish()

    def norm_to_hT(l, k, shift0, have_ssq=False):
        ssq_rstd(have_ssq)
        for c in range(KC):
            tmp = T[2 + c % 2]
            S.op("dve", lambda e, c=c, tmp=tmp: e.tensor_tensor(out=tmp, in0=xT[:, c, :], in1=RSTD, op=ALU.mult),
                 r=[("xT", c), "RSTD"], w=[("T", 2 + c % 2)])
            S.op("act", lambda e, c=c, tmp=tmp: e.activation(out=hT[:, c, :], in_=tmp, func=AF.Identity,
                                                             scale=nrm[:, l, k, c:c + 1],
                                                             bias=modT[:, l, shift0 + c:shift0 + c + 1]),
                 r=[("T", 2 + c % 2), "nrm", "modT"], w=["hT"])

    XSA = carve(24576, 4096, F32).rearrange("p (c v) -> p c v", v=128)
    Hh = carve(24576 + 4096, 6144, F32)[:, 0:9 * 128].rearrange("p (c v) -> p c v", v=128)
    XK = [("T", 0), ("T", 1)]
    HK = [("T", 2), ("T", 3), ("T", 4)]

    def evec_ec(r0, r1, kind, hslot):
        return EC[r0:r1, kind, hslot, :]

    def evec_const(col):
        return bass.AP(consts[:].tensor, C_COLS + col, [[consts[:].ap[0][0], 128], [0, 8]])

    def bc_half(vec, half):
        st = vec.ap[1][0]
        return bass.AP(vec.tensor, vec.offset + half * st, [[vec.ap[0][0], vec.ap[0][1]], [2 * st, 4], [0, 128]])

    def bc_all(vec):
        st = vec.ap[1][0]
        return bass.AP(vec.tensor, vec.offset, [[vec.ap[0][0], vec.ap[0][1]], [st, 8], [0, 128]])

    def head_A(l, buf, kbuf, kt_tile, q_tile, rows, v_cols, st_head, evecs, do_kT=True):
        r0, r1 = rows
        e_mid, e_last, e_lm = evecs
        ktt = KTT[kbuf]
        if do_kT:
            def fn(e):
                ins = None
                pv = pb[7][:].bitcast(BF16)
                for tg in range(4):
                    ins = e.transpose(out=pv[:, tg * 128:(tg + 1) * 128], in_=kt_tile[:, tg * 128:(tg + 1) * 128], identity=identb[:])
                return ins
            S.op("pe", fn, r=["KT", "identb"], w=[pbk(7)])
            copy_op("act", ktt.rearrange("p g c -> p (g c)"), pb[7][:].bitcast(BF16)[:, 0:512], [pbk(7)], [("KTT", kbuf)])
        def fn(e):
            ins = None
            for tg in range(4):
                ins = e.matmul(pb[3][:, tg * 128:(tg + 1) * 128], lhsT=kt_tile[r0:r1, tg * 128:(tg + 1) * 128],
                               rhs=q_tile[r0:r1, tg * 128:(tg + 1) * 128], start=True, stop=True)
            return ins
        S.op("pe", fn, r=["KT", "QT"], w=[pbk(3)])
        at = ATb[buf]
        mI = cc(C_MASKA).bitcast(I32)
        S.op("dve", lambda e: e.copy_predicated(out=at.rearrange("p (g t) -> p g t", t=128),
                                                mask=bass.AP(mI.tensor, mI.offset, [[mI.ap[0][0], 128], [0, 4], [1, 128]]),
                                                data=pb[3][:].rearrange("p (g t) -> p g t", t=128)),
             r=[pbk(3), "consts", ("AT", buf)], w=[("AT", buf)])
        def fn(e):
            ins = None
            for c in range(8):
                tg, half = c // 2, c % 2
                ins = e.matmul(pb[4 + half][r0:r1, tg * 128:(tg + 1) * 128],
                               lhsT=ktt[half * 64:(half + 1) * 64, tg, r0:r1],
                               rhs=VTM[half * 64:(half + 1) * 64, tg, v_cols[0]:v_cols[1]], start=True, stop=True)
            return ins
        S.op("pe", fn, r=[("KTT", kbuf), "VTM"], w=[pbk(4), pbk(5)])
        sbf = SBF[buf]
        Sv = ST[r0:r1, l, st_head, :]
        stk = ("ST", l, st_head)
        S.op("act", lambda e: e.activation(out=Hh[r0:r1, 0, :], in_=Sv, func=AF.Copy), r=[stk], w=HK)
        xs4 = XSA.rearrange("p (g h) v -> p g h v", h=2)
        for half in range(2):
            S.op("dve", lambda e, half=half: e.tensor_tensor(out=xs4[r0:r1, :, half, :],
                                                             in0=pb[4 + half][r0:r1, :].rearrange("p (g v) -> p g v", v=128),
                                                             in1=bc_half(e_lm, half), op=ALU.mult),
                 r=[pbk(4 + half), "EC"], w=XK)
        for c in range(8):
            S.op("dve", lambda e, c=c: e.scalar_tensor_tensor(out=Hh[r0:r1, c + 1, :], in0=Hh[r0:r1, c, :], scalar=e_last[:, c:c + 1],
                                                              in1=XSA[r0:r1, c, :], op0=ALU.mult, op1=ALU.add),
                 r=HK + XK + ["EC"], w=HK)
        S.op("dve", lambda e: e.tensor_tensor(out=sbf[r0:r1, :, :], in0=Hh[r0:r1, 0:8, :], in1=bc_all(e_mid), op=ALU.mult),
             r=HK + ["EC"], w=[("SBF", buf)])
        S.op("act", lambda e: e.activation(out=Sv, in_=Hh[r0:r1, 8, :], func=AF.Copy), r=HK, w=[stk])

    def head_C(l, buf, q_tile, rows, v_cols, nw_col, gate_idx, mix_chunk, part=0):
      r0, r1 = rows
      at = ATb[buf]
      sbf = SBF[buf]
      if part in (0, 1):
        def fn(e):
            ins = None
            for tg in range(4):
                e.matmul(pb[6][:, tg * 128:(tg + 1) * 128], lhsT=VTM[:, tg, v_cols[0]:v_cols[1]],
                         rhs=at[:, tg * 128:(tg + 1) * 128], start=True, stop=False)
                for half in range(2):
                    c = 2 * tg + half
                    ins = e.matmul(pb[6][:, c * 64:(c + 1) * 64], lhsT=sbf[r0:r1, c, :], rhs=q_tile[r0:r1, c * 64:(c + 1) * 64],
                                   start=False, stop=(half == 1))
            return ins
        S.op("pe", fn, r=["VTM", ("AT", buf), ("SBF", buf), "QT"], w=[pbk(6)])
        sqb = SQb.bitcast(BF16)[:, 0:TT]
        S.op("act", lambda e: e.activation(out=sqb, in_=pb[6][:], func=AF.Square), r=[pbk(6)], w=["SQb"])
        if part == 1:
            return
      if True:
        sqb = SQb.bitcast(BF16)[:, 0:TT]
        S.op("pe", lambda e: e.matmul(pb[7][:], lhsT=onesb[:], rhs=sqb, start=True, stop=True), r=["SQb", "onesb"], w=[pbk(7)])
        S.op("act", lambda e: e.activation(out=RSTD, in_=pb[7][:], func=AF.Ln, scale=1.0 / 128, bias=EPS), r=[pbk(7)], w=["RSTD"])
        S.op("act", lambda e: e.activation(out=RSTD, in_=RSTD, func=AF.Exp, scale=-0.5), r=["RSTD"], w=["RSTD"])
        S.op("dve", lambda e: e.scalar_tensor_tensor(out=TTb, in0=pb[6][:], scalar=(nw_col if nw_col is not None else 1.0),
                                                     in1=RSTD, op0=ALU.mult, op1=ALU.mult),
             r=[pbk(6), "RSTD", "vT"], w=["TTb"])
        S.op("dve", lambda e: e.tensor_tensor(out=mixT[:, mix_chunk, :], in0=TTb, in1=GATE[:, gate_idx, :], op=ALU.mult),
             r=["TTb", "GATE"], w=[("mixT", mix_chunk)])

    def run_fill(fillers, n):
        for _ in range(n):
            if fillers:
                fillers.pop(0)()

    def run_heads(specs, fillers=None):
        fillers = fillers if fillers is not None else []
        n = len(specs)
        A = lambda i: head_A(*specs[i][0])
        C1 = lambda i: head_C(*specs[i][1], part=1)
        C2 = lambda i: head_C(*specs[i][1], part=2)
        A(0)
        run_fill(fillers, 1)
        for i in range(n):
            if i + 1 < n:
                A(i + 1)
                run_fill(fillers, 1)
            if i > 0:
                C2(i - 1)
                run_fill(fillers, 1)
            C1(i)
            run_fill(fillers, 1)
        run_fill(fillers, 2)
        C2(n - 1)
        run_fill(fillers, len(fillers))

    def decay_tables(bsrc_tile, key_b, scale_q, hslot, tD, tEk, keyD, keyEk):
        b3 = bsrc_tile.rearrange("p (c j) -> p c j", j=64)
        d3 = tD.rearrange("p (c j) -> p c j", j=64)
        S.op("dve", lambda e: e.tensor_tensor(out=d3, in0=b3, in1=b3[:, :, 31:32].to_broadcast([128, 8, 64]), op=ALU.subtract),
             r=[key_b], w=[keyD])
        S.op("act", lambda e: e.activation(out=EQ[:, hslot, :], in_=tD, func=AF.Exp, scale=scale_q), r=[keyD], w=EQK)
        S.op("act", lambda e: e.activation(out=tEk, in_=tD, func=AF.Exp, scale=-scale_q), r=[keyD], w=[keyEk])
        S.op("act", lambda e: e.activation(out=EC[:, 0, hslot, :], in_=b3[:, :, 31], func=AF.Exp, scale=scale_q), r=[key_b], w=["EC"])
        S.op("act", lambda e: e.activation(out=EC[:, 1, hslot, :], in_=b3[:, :, 63], func=AF.Exp, scale=scale_q), r=[key_b], w=["EC"])
        S.op("act", lambda e: e.activation(out=EC[:, 2, hslot, :], in_=d3[:, :, 63], func=AF.Exp, scale=scale_q), r=[keyD], w=["EC"])

    def rotate_from_psum(bank, perm_c0, cos_t, sin_t, out_fn):
        S.op("act", lambda e: e.activation(out=T[0], in_=pb[bank][:], func=AF.Copy), r=[pbk(bank)], w=[("T", 0)])
        S.op("pe", lambda e: e.matmul(pb[7][:], lhsT=cc(perm_c0), rhs=T[0], start=True, stop=True),
             r=[("T", 0), "consts"], w=[pbk(7)])
        S.op("dve", lambda e: e.tensor_tensor(out=T[1], in0=T[0], in1=cos_t, op=ALU.mult), r=[("T", 0), "rot"], w=[("T", 1)])
        S.op("dve", lambda e: e.tensor_tensor(out=T[2], in0=pb[7][:], in1=sin_t, op=ALU.mult), r=[pbk(7), "rot"], w=[("T", 2)])
        S.op("dve", lambda e: e.tensor_tensor(out=T[1], in0=T[1], in1=T[2], op=ALU.add), r=[("T", 1), ("T", 2)], w=[("T", 1)])
        out_fn(T[1])

    cosR, sinR, cosS, sinS = rot[:, 0, :], rot[:, 1, :], rot[:, 2, :], rot[:, 3, :]

    def bc_chunks(c0):
        return bass.AP(consts[:].tensor, c0, [[consts[:].ap[0][0], 128], [0, 8], [1, 64]])

    XST = carve(0, 32768, F32).rearrange("p (g n) -> p g n", n=D)
    XSTK = [list(EQK), ["QT", "KT"], ["VTM", "GATE"], [("T", i) for i in range(4)]]

    def issue_xload(tile):
        for g in range(4):
            S.dma("sp", ringS, XST[:, g, :], x_d[tile * TT + g * 128:tile * TT + (g + 1) * 128, :], w=XSTK[g])

    try:
        chk("setup")
        for tt in range(NT):
            S.epoch = tt + 1
            t0 = tt * TT
            if tt == 0:
                issue_xload(0)
            for g in range(4):
                for cq in range(4):
                    bank = pbank()

                    def fn(e, g=g, cq=cq, bank=bank):
                        ins = None
                        for j in range(4):
                            c = cq * 4 + j
                            ins = e.transpose(out=pb[bank][:, j * 128:(j + 1) * 128], in_=XST[:, g, c * 128:(c + 1) * 128], identity=identf)
                        return ins
                    S.op("pe", fn, r=XSTK[g] + ["consts"], w=[pbk(bank)])
                    copy_op(cpeng(), xT[:, cq * 4:cq * 4 + 4, g * 128:(g + 1) * 128],
                            pb[bank][:].rearrange("p (j t) -> p j t", t=128), [pbk(bank)], [("xT", cq * 4 + j) for j in range(4)])
            chk("xload")
            S.dma("sp", ringS, posi, bass.AP(pos_d.tensor, t0, [[0, 128], [1, TT]]), w=[("T", 3)])
            S.op("dve", lambda e: e.tensor_copy(out=posf, in_=posi), r=[("T", 3)], w=[("T", 4)])
            for ti, (icol, scol) in enumerate(((0, 1), (2, 3))):
                for which in range(2):
                    dst = rot[:, ti * 2 + (1 - which), :]
                    add = 0.0 if which == 0 else math.pi / 2
                    S.op("dve", lambda e, icol=icol, add=add: e.tensor_scalar(out=T[0], in0=posf, scalar1=ccol(icol), scalar2=add,
                                                                              op0=ALU.mult, op1=ALU.add), r=[("T", 4), "consts"], w=[("T", 0)])
                    S.op("dve", lambda e: e.tensor_scalar(out=rti, in0=T[0], scalar1=1.0 / TWO_PI, scalar2=None, op0=ALU.mult),
                         r=[("T", 0)], w=[("T", 2)])
                    S.op("dve", lambda e: e.tensor_copy(out=T[1], in_=rti), r=[("T", 2)], w=[("T", 1)])
                    S.op("dve", lambda e: e.scalar_tensor_tensor(out=T[1], in0=T[1], scalar=-TWO_PI, in1=T[0],
                                                                 op0=ALU.mult, op1=ALU.add), r=[("T", 0), ("T", 1)], w=[("T", 1)])
                    if which == 0:
                        S.op("act", lambda e, dst=dst, scol=scol: e.activation(out=dst, in_=T[1], func=AF.Sin, scale=ccol(scol)),
                             r=[("T", 1), "consts"], w=["rot"])
                    else:
                        S.op("act", lambda e, dst=dst: e.activation(out=dst, in_=T[1], func=AF.Sin), r=[("T", 1)], w=["rot"])
            chk("rot")
            if tt == 0:
                tap("rot", rot[:].rearrange("p a t -> p (a t)"), [128, 4 * TT], ["rot"])
                tap("xT", xT[:].rearrange("p c t -> p (c t)"), [128, KC * TT], [("xT", c) for c in range(KC)])

            def layer_body(l):
                vb = l * V_LAYER
                rr["scratch"], rr["layer"], rr["tt"], rr["sidx"] = True, l, tt, -1
                for i in range(2):
                    S.op("pool", lambda e, i=i: e.memset(ATb[i], 0.0), w=[("AT", i)])
                norm_to_hT(l, 0, 0, have_ssq=(l > 0))
                if tt == 0:
                    tap("h1_%d" % l, hT[:].rearrange("p c t -> p (c t)"), [128, KC * TT], ["hT"])
                win = win_d[l]
                chk("norm1")

                wo_buf = stage[:].rearrange("p a n -> p (a n)").bitcast(BF16).rearrange("p (k n) -> p k n", n=D)
                WOK = [("stage", 0), ("stage", 1)]

                def outproj_partial(g, last):
                    wo_flat = stage[:].rearrange("p a n -> p (a n)").bitcast(BF16)
                    if tt <= l:
                        S.dma("pool", ringW, wo_buf, wout_d[l][g * 512:(g + 1) * 512, :].rearrange("(k p) n -> p k n", p=128), w=WOK)
                        if tt == l:
                            S.dma("sp", ringB, wsc[l, 60 + g], wo_flat, r=WOK, w=[("wsc", l, 60 + g)])
                    else:
                        S.dma("pool", ringW, wo_flat, wsc[l, 60 + g], r=[("wsc", l, 60 + g)], w=WOK)
                    fl = []
                    for c in range(KC):
                        def f(c=c):
                            bank = pbank()

                            def fn(e):
                                ins = None
                                for kc in range(4):
                                    ins = e.matmul(pb[bank][:, :], lhsT=wo_buf[:, kc, c * 128:(c + 1) * 128], rhs=mixT[:, 4 * g + kc, :],
                                                   start=(kc == 0), stop=(kc == 3))
                                return ins
                            S.op("pe", fn, r=WOK + [("mixT", 4 * g + k) for k in range(4)], w=[pbk(bank)])
                            S.op("dve", lambda e: e.scalar_tensor_tensor(out=xT[:, c, :], in0=pb[bank][:], scalar=modT[:, l, 32 + c:33 + c],
                                                                         in1=xT[:, c, :], op0=ALU.mult, op1=ALU.add),
                                 r=[pbk(bank), "modT", ("xT", c)], w=[("xT", c)])
                            if last:
                                ssq_chunk(c, "T", defer=True)
                        fl.append(f)
                    return fl

                ar_sl, ar_k = load_slab(win, 1536, 16, KC)
                bank = pbank()
                proj_fm(ar_sl, ar_k, 0, 16, bank, hT, ["hT"])
                arT = TTb.bitcast(BF16)[0:16, 0:TT]
                S.op("dve", lambda e, bank=bank: e.tensor_copy(out=arT, in_=pb[bank][0:16, :]), r=[pbk(bank)], w=["TTb"])
                qk_sl, qk_k = load_slab(win, 0, 512, KC)
                for pt in range(2):
                    bank = pbank()
                    S.op("pe", lambda e, pt=pt, bank=bank: e.matmul(pb[bank][:], lhsT=w2b[0:16, l, pt * 128:(pt + 1) * 128], rhs=arT,
                                                                    start=True, stop=True), r=["w2b", "TTb"], w=[pbk(bank)])
                    S.op("act", lambda e, pt=pt, bank=bank: e.activation(out=T[0], in_=pb[bank][:], func=AF.Exp, scale=-1.0,
                                                                          bias=smallc[:, NB2 + 2 * l + pt:NB2 + 2 * l + pt + 1]),
                         r=[pbk(bank), "smallc"], w=[("T", 0)])
                    S.op("act", lambda e: e.activation(out=T[0], in_=T[0], func=AF.Ln, bias=1.0), r=[("T", 0)], w=[("T", 0)])
                    S.op("dve", lambda e: e.tensor_tensor_scan(out=T[1], data0=msk[:], data1=T[0], initial=0.0, op0=ALU.mult, op1=ALU.add),
                         r=[("T", 0), "msk"], w=[("T", 1)])
                    decay_tables(T[1], ("T", 1), -1.0 / 16.0, pt, T[2], T[3], ("T", 2), ("T", 3))
                    bank = pbank()
                    proj_fm(qk_sl, qk_k, pt * 128, 128, bank, hT, ["hT"])
                    S.op("dve", lambda e, pt=pt, bank=bank: e.scalar_tensor_tensor(out=QT[:, pt, :], in0=pb[bank][:], scalar=0.125,
                                                                                   in1=EQ[:, pt, :], op0=ALU.mult, op1=ALU.mult),
                         r=[pbk(bank)] + EQK, w=["QT"])
                    bank = pbank()
                    proj_fm(qk_sl, qk_k, 256 + pt * 128, 128, bank, hT, ["hT"])
                    S.op("dve", lambda e, pt=pt, bank=bank: e.tensor_tensor(out=KT[:, pt, :], in0=pb[bank][:], in1=T[3], op=ALU.mult),
                         r=[pbk(bank), ("T", 3)], w=["KT"])
                if tt == 0:
                    tap("gla_q_%d" % l, QT[:, 0:2, :].rearrange("p h t -> p (h t)"), [128, 2 * TT], ["QT"])
                    tap("gla_k_%d" % l, KT[:, 0:2, :].rearrange("p h t -> p (h t)"), [128, 2 * TT], ["KT"])
                    tap("gla_ec_%d" % l, EC.rearrange("p k h c -> p (k h c)"), [128, 96], ["EC"])

                chk("gla_proj")

                def v_and_gate(c_v, c_g):
                    v_sl, v_k = load_slab(win, c_v, 512, KC)
                    for tg in range(4):
                        bank = pbank()
                        proj_tm(v_sl, v_k, 0, 512, tg, bank)
                        copy_op("act", VTM[:, tg, :], pb[bank][:], [pbk(bank)], ["VTM"])
                    g_sl, g_k = load_slab(win, c_g, 512, KC)
                    for j in range(4):
                        bank = pbank()
                        proj_fm(g_sl, g_k, j * 128, 128, bank, hT, ["hT"])
                        S.op("act", lambda e, j=j, bank=bank: e.activation(out=GATE[:, j, :], in_=pb[bank][:], func=AF.Silu),
                             r=[pbk(bank)], w=["GATE"])

                v_and_gate(512, 1024)
                specs = []
                for h in range(4):
                    pt, hp = h // 2, h % 2
                    rows = (hp * 64, hp * 64 + 64)
                    vc = (h * 128, h * 128 + 128)
                    ev = tuple(evec_ec(rows[0], rows[1], kind, pt) for kind in range(3))
                    specs.append(((l, h % 2, pt, KT[:, pt, :], QT[:, pt, :], rows, vc, h, ev, hp == 0),
                                  (l, h % 2, QT[:, pt, :], rows, vc, vT[:, vb + V_GNW:vb + V_GNW + 1], h, h)))
                run_heads(specs)
                if tt == 0:
                    tap("mix_gla_%d" % l, mixT[:, 0:4, :].rearrange("p c t -> p (c t)"), [128, 4 * TT], [("mixT", c) for c in range(4)])

                chk("gla")
                for which, c0 in ((0, 1552), (1, 2064)):
                    sl, slk = load_slab(win, c0, 512, KC)
                    for h in range(4):
                        bank = pbank()
                        proj_fm(sl, slk, h * 128, 128, bank, hT, ["hT"])
                        if which == 0:
                            rotate_from_psum(bank, C_PERM_RET, cosR, sinR, lambda t, h=h: S.op(
                                "dve", lambda e: e.scalar_tensor_tensor(out=QT[:, h, :].rearrange("p (c j) -> p c j", j=64),
                                                                        in0=t.rearrange("p (c j) -> p c j", j=64), scalar=128.0 ** -0.5,
                                                                        in1=bc_chunks(C_RET_EQ + h * 64), op0=ALU.mult, op1=ALU.mult),
                                r=[("T", 1), "consts"], w=["QT"]))
                        else:
                            rotate_from_psum(bank, C_PERM_RET, cosR, sinR, lambda t, h=h: S.op(
                                "dve", lambda e: e.tensor_tensor(out=KT[:, h, :].rearrange("p (c j) -> p c j", j=64),
                                                                 in0=t.rearrange("p (c j) -> p c j", j=64),
                                                                 in1=bc_chunks(C_RET_EK + h * 64), op=ALU.mult),
                                r=[("T", 1), "consts"], w=["KT"]))
                v_and_gate(2576, 3088)
                specs = []
                for h in range(4):
                    vc = (h * 128, h * 128 + 128)
                    ev = tuple(evec_const(4 + 3 * h + kind) for kind in range(3))
                    specs.append(((l, h % 2, h % 2, KT[:, h, :], QT[:, h, :], (0, 128), vc, 4 + h, ev, True),
                                  (l, h % 2, QT[:, h, :], (0, 128), vc, None, h, 4 + h)))
                fill = outproj_partial(0, False)
                run_heads(specs, fill)
                if tt == 0:
                    tap("mix_ret_%d" % l, mixT[:, 4:8, :].rearrange("p c t -> p (c t)"), [128, 4 * TT], [("mixT", c) for c in range(4, 8)])

                chk("ret")
                sl, slk = load_slab(win, 3600, 512, KC)
                for j in range(4):
                    bank = pbank()
                    proj_fm(sl, slk, j * 128, 128, bank, hT, ["hT"])
                    rotate_from_psum(bank, C_PERM_SWQ, cosS, sinS, lambda t, j=j: S.op(
                        "act", lambda e: e.activation(out=QT[:, j, :], in_=t, func=AF.Copy), r=[("T", 1)], w=["QT"]))
                sl, slk = load_slab(win, 4112, 256, KC)
                bank = pbank()
                proj_fm(sl, slk, 0, 128, bank, hT, ["hT"])
                S.op("act", lambda e, bank=bank: e.activation(out=T[0], in_=pb[bank][:], func=AF.Copy), r=[pbk(bank)], w=[("T", 0)])
                for kvh in range(2):
                    S.op("pe", lambda e, kvh=kvh: e.matmul(pb[6][:], lhsT=cc(C_DUP + (2 * kvh) * 128), rhs=T[0], start=True, stop=True),
                         r=[("T", 0), "consts"], w=[pbk(6)])
                    S.op("pe", lambda e, kvh=kvh: e.matmul(pb[7][:], lhsT=cc(C_DUP + (2 * kvh + 1) * 128), rhs=T[0], start=True, stop=True),
                         r=[("T", 0), "consts"], w=[pbk(7)])
                    S.op("dve", lambda e: e.tensor_tensor(out=T[1], in0=pb[6][:], in1=cosS, op=ALU.mult), r=[pbk(6), "rot"], w=[("T", 1)])
                    S.op("dve", lambda e: e.tensor_tensor(out=T[2], in0=pb[7][:], in1=sinS, op=ALU.mult), r=[pbk(7), "rot"], w=[("T", 2)])
                    S.op("dve", lambda e, kvh=kvh: e.tensor_tensor(out=SKD[:, l, kvh, 128:640], in0=T[1], in1=T[2], op=ALU.add),
                         r=[("T", 1), ("T", 2)], w=[("SKD", l)])
                for tg in range(4):
                    bank = pbank()
                    proj_tm(sl, slk, 128, 128, tg, bank)
                    copy_op(cpeng(), SV[:, l, 1 + tg, :], pb[bank][:, 0:128], [pbk(bank)], [("SV", l)])
                SCs = [T12.rearrange("p (g k) -> p g k", k=256), carve(0, 4096, F32).rearrange("p (g k) -> p g k", k=256)]
                PBs = [T34b[:, 0:1024].rearrange("p (g k) -> p g k", k=256), carve(4096, 2048, BF16).rearrange("p (g k) -> p g k", k=256)]
                PTss = [T34b[:, 1024:2048].rearrange("p (a q) -> p a q", q=128), carve(6144, 2048, BF16).rearrange("p (a q) -> p a q", q=128)]
                SCK = [[("T", 0), ("T", 1)], [("EQ", 0)]]
                PBK = [[("T", 2)], [("EQ", 1)]]
                PTK = [[("T", 3)], [("EQ", 2)]]
                CTMs = [T[4].bitcast(BF16)[:, 0:512], T[4].bitcast(BF16)[:, 512:1024]]
                ctm_pending = []
                its = [(blk, kvh) for blk in range(4) for kvh in range(2)]

                def swa_stage1(i):
                    blk, kvh = its[i]
                    k = i % 2
                    SC, PB_ = SCs[k], PBs[k]
                    STAT = TTb[:, 32 * k:32 * k + 16]
                    maskc = C_SWAMASK0 if (tt == 0 and blk == 0) else C_SWAMASK

                    def fn(e):
                        ins = None
                        for g in range(4):
                            h = kvh * 4 + g
                            j, hp = h // 2, h % 2
                            ins = e.matmul(pb[4 + g % 2][:, (g // 2) * 256:(g // 2 + 1) * 256],
                                           lhsT=QT[hp * 64:hp * 64 + 64, j, blk * 128:(blk + 1) * 128],
                                           rhs=SKD[hp * 64:hp * 64 + 64, l, kvh, blk * 128:blk * 128 + 256], start=True, stop=True)
                        return ins
                    S.op("pe", fn, r=["QT", ("SKD", l)], w=[pbk(4), pbk(5)])

                def swa_stage1b(i):
                    blk, kvh = its[i]
                    k = i % 2
                    SC, PB_ = SCs[k], PBs[k]
                    STAT = TTb[:, 32 * k:32 * k + 16]
                    maskc = C_SWAMASK0 if (tt == 0 and blk == 0) else C_SWAMASK
                    for hb in range(2):
                        S.op("dve", lambda e, hb=hb: e.scalar_tensor_tensor(
                            out=SC[:, hb::2, :], in0=pb[4 + hb][:].rearrange("p (g k) -> p g k", k=256), scalar=0.125,
                            in1=bass.AP(consts[:].tensor, maskc, [[consts[:].ap[0][0], 128], [0, 2], [1, 256]]),
                            op0=ALU.mult, op1=ALU.add), r=[pbk(4 + hb), "consts"], w=SCK[k])
                    MX, RS, ESn, RINV = STAT[:, 0:4], STAT[:, 4:8], STAT[:, 8:12], STAT[:, 12:16]
                    sk = sinkb[:, l * 8 + kvh * 4:l * 8 + kvh * 4 + 4]
                    S.op("dve", lambda e: e.tensor_reduce(out=MX, in_=SC, axis=AX.X, op=ALU.max), r=SCK[k], w=["TTb"])
                    S.op("dve", lambda e: e.tensor_tensor(out=MX, in0=MX, in1=sk, op=ALU.max), r=["TTb", "sinkb"], w=["TTb"])
                    NMX = TTb[:, 32 * k + 16:32 * k + 20]
                    S.op("dve", lambda e: e.tensor_scalar(out=NMX, in0=MX, scalar1=-1.0, scalar2=None, op0=ALU.mult), r=["TTb"], w=["TTb"])

                    def fn_exp(e):
                        ins = None
                        for g in range(4):
                            ins = e.activation(out=PB_[:, g, :], in_=SC[:, g, :], func=AF.Exp, bias=NMX[:, g:g + 1], accum_out=RS[:, g:g + 1])
                        return ins
                    S.op("act", fn_exp, r=SCK[k] + ["TTb"], w=PBK[k] + ["TTb"])
                    S.op("dve", lambda e: e.tensor_tensor(out=ESn, in0=sk, in1=MX, op=ALU.subtract), r=["TTb", "sinkb"], w=["TTb"])
                    S.op("act", lambda e: e.activation(out=ESn, in_=ESn, func=AF.Exp), r=["TTb"], w=["TTb"])
                    S.op("dve", lambda e: e.tensor_tensor(out=RS, in0=RS, in1=ESn, op=ALU.add), r=["TTb"], w=["TTb"])
                    S.op("dve", lambda e: e.reciprocal(out=RINV, in_=RS), r=["TTb"], w=["TTb"])

                def swa_stage2(i):
                    blk, kvh = its[i]
                    k = i % 2
                    PB_, PTs = PBs[k], PTss[k]
                    RINV = TTb[:, 32 * k + 12:32 * k + 16]

                    def fn(e):
                        ins = None
                        pv = pb[3][:].bitcast(BF16)
                        for g in range(4):
                            for kb in range(2):
                                a_ = g * 2 + kb
                                ins = e.transpose(out=pv[:, a_ * 128:(a_ + 1) * 128], in_=PB_[:, g, kb * 128:(kb + 1) * 128], identity=identb[:])
                        return ins
                    S.op("pe", fn, r=PBK[k] + ["identb"], w=[pbk(3)])
                    pv3 = pb[3][:].bitcast(BF16)
                    copy_op("act", PTs.rearrange("p a q -> p (a q)"), pv3[:, 0:1024], [pbk(3)], PTK[k])

                def swa_stage2b(i):
                    blk, kvh = its[i]
                    k = i % 2
                    PB_, PTs = PBs[k], PTss[k]
                    RINV = TTb[:, 32 * k + 12:32 * k + 16]
                    CTM = CTMs[blk % 2]

                    def fn(e):
                        ins = None
                        for g in range(4):
                            e.matmul(pb[6][:, g * 64:(g + 1) * 64], lhsT=PTs[:, g * 2, :], rhs=SV[:, l, blk, kvh * 64:(kvh + 1) * 64],
                                     start=True, stop=False)
                            ins = e.matmul(pb[6][:, g * 64:(g + 1) * 64], lhsT=PTs[:, g * 2 + 1, :], rhs=SV[:, l, blk + 1, kvh * 64:(kvh + 1) * 64],
                                           start=False, stop=True)
                        return ins
                    S.op("pe", fn, r=PTK[k] + [("SV", l)], w=[pbk(6)])
                    S.op("dve", lambda e: e.tensor_tensor(
                        out=CTM[:, kvh * 256:(kvh + 1) * 256].rearrange("p (g d) -> p g d", d=64),
                        in0=pb[6][:, 0:256].rearrange("p (g d) -> p g d", d=64),
                        in1=bass.AP(RINV.tensor, RINV.offset, [[RINV.ap[0][0], 128], [1, 4], [0, 64]]), op=ALU.mult),
                        r=[pbk(6), "TTb"], w=[("T", 4)])
                    if kvh == 1:
                        def ctm_out():
                            def fn(e):
                                ins = None
                                pv = pb[7][:].bitcast(BF16)
                                for j in range(4):
                                    ins = e.transpose(out=pv[:, j * 128:(j + 1) * 128], in_=CTM[:, j * 128:(j + 1) * 128], identity=identb[:])
                                return ins
                            S.op("pe", fn, r=[("T", 4), "identb"], w=[pbk(7)])
                            copy_op("act", mixT[:, 8:12, blk * 128:(blk + 1) * 128],
                                    pb[7][:].bitcast(BF16)[:, 0:512].rearrange("p (j q) -> p j q", q=128), [pbk(7)], [("mixT", 8 + j) for j in range(4)])
                        ctm_pending.append(ctm_out)

                fill = outproj_partial(1, False)
                swa_stage1(0)
                swa_stage1b(0)
                for i in range(8):
                    if i + 1 < 8:
                        swa_stage1(i + 1)
                    swa_stage2(i)
                    while ctm_pending:
                        ctm_pending.pop(0)()
                    if i + 1 < 8:
                        swa_stage1b(i + 1)
                    run_fill(fill, 1)
                    swa_stage2b(i)
                    run_fill(fill, 1)
                while ctm_pending:
                    ctm_pending.pop(0)()
                run_fill(fill, len(fill))
                S.op("pool", lambda e: e.tensor_copy(out=SKD[:, l, :, 0:128], in_=SKD[:, l, :, 512:640]), r=[("SKD", l)], w=[("SKD", l)])
                S.op("pool", lambda e: e.tensor_copy(out=SV[:, l, 0, :], in_=SV[:, l, 4, :]), r=[("SV", l)], w=[("SV", l)])
                if tt == 0:
                    tap("mix_swa_%d" % l, mixT[:, 8:12, :].rearrange("p c t -> p (c t)"), [128, 4 * TT], [("mixT", c) for c in range(8, 12)])

                chk("swa")
                f_sl, f_k = load_slab(win, 4880, 512, KC)
                for h in range(4):
                    bank = pbank()
                    proj_fm(f_sl, f_k, h * 128, 128, bank, hT, ["hT"])
                    lbc = smallc[:, LB0 + l * 4 + h:LB0 + l * 4 + h + 1]
                    omc = smallc[:, OML0 + l * 4 + h:OML0 + l * 4 + h + 1]
                    S.op("act", lambda e, bank=bank: e.activation(out=T[0], in_=pb[bank][:], func=AF.Exp, scale=-1.0), r=[pbk(bank)], w=[("T", 0)])
                    S.op("dve", lambda e: e.tensor_scalar(out=T[0], in0=T[0], scalar1=1.0, scalar2=None, op0=ALU.add), r=[("T", 0)], w=[("T", 0)])
                    S.op("dve", lambda e: e.reciprocal(out=T[0], in_=T[0]), r=[("T", 0)], w=[("T", 0)])
                    S.op("act", lambda e, lbc=lbc, omc=omc: e.activation(out=T[0], in_=T[0], func=AF.Identity, scale=omc, bias=lbc),
                         r=[("T", 0), "smallc"], w=[("T", 0)])
                    S.op("act", lambda e: e.activation(out=T[1], in_=T[0], func=AF.Ln), r=[("T", 0)], w=[("T", 1)])
                    S.op("dve", lambda e: e.tensor_tensor_scan(out=T[2], data0=msk[:], data1=T[1], initial=0.0, op0=ALU.mult, op1=ALU.add),
                         r=[("T", 1), "msk"], w=[("T", 2)])
                    S.op("act", lambda e: e.activation(out=T[0], in_=T[0], func=AF.Identity, scale=-1.0, bias=1.0),
                         r=[("T", 0)], w=[("T", 0)])
                    decay_tables(T[2], ("T", 2), 1.0, h, T[3], T[4], ("T", 3), ("T", 4))
                    S.op("dve", lambda e, h=h: e.tensor_tensor(out=KT[:, h, :], in0=T[0], in1=T[4], op=ALU.mult), r=[("T", 0), ("T", 4)], w=["KT"])
                q_sl, q_k = load_slab(win, 4368, 512, KC)
                for h in range(4):
                    bank = pbank()
                    proj_fm(q_sl, q_k, h * 128, 128, bank, hT, ["hT"])
                    S.op("act", lambda e, bank=bank: e.activation(out=T[0], in_=pb[bank][:], func=AF.Silu), r=[pbk(bank)], w=[("T", 0)])
                    S.op("dve", lambda e, h=h: e.scalar_tensor_tensor(out=QT[:, h, :], in0=T[0], scalar=128.0 ** -0.5, in1=EQ[:, h, :],
                                                                      op0=ALU.mult, op1=ALU.mult), r=[("T", 0)] + EQK, w=["QT"])
                if tt == 0:
                    tap("hg_q_%d" % l, QT.rearrange("p h t -> p (h t)"), [128, 4 * TT], ["QT"])
                    tap("hg_k_%d" % l, KT.rearrange("p h t -> p (h t)"), [128, 4 * TT], ["KT"])
                    tap("hg_ec_%d" % l, EC.rearrange("p k h c -> p (k h c)"), [128, 96], ["EC"])
                    tap("hg_eq_%d" % l, EQ.rearrange("p h t -> p (h t)"), [128, 4 * TT], EQK)
                    tap("smallc_%d" % l, smallc[:], [128, 64], ["smallc"])
                v_and_gate(5392, 5904)
                specs = []
                for h in range(4):
                    vc = (h * 128, h * 128 + 128)
                    ev = tuple(evec_ec(0, 128, kind, h) for kind in range(3))
                    specs.append(((l, h % 2, h % 2, KT[:, h, :], QT[:, h, :], (0, 128), vc, 8 + h, ev, True),
                                  (l, h % 2, QT[:, h, :], (0, 128), vc, vT[:, vb + V_HNW:vb + V_HNW + 1], h, 12 + h)))
                fill = outproj_partial(2, False)
                run_heads(specs, fill)
                fill = outproj_partial(3, True)
                run_fill(fill, len(fill))
                if tt == 0:
                    tap("mix_hg_%d" % l, mixT[:, 12:16, :].rearrange("p c t -> p (c t)"), [128, 4 * TT], [("mixT", c) for c in range(12, 16)])

                chk("hg")
                if tt == 0:
                    tap("x_mid_%d" % l, xT[:].rearrange("p c t -> p (c t)"), [128, KC * TT], [("xT", c) for c in range(KC)])

                chk("outproj")
                norm_to_hT(l, 1, 48, have_ssq=True)
                barrier()
                for s in range(22):
                    g_sl, g_k = load_slab(wffi_d[l], s * 256, 256, KC, half=0)
                    u_sl, u_k = load_slab(wffi_d[l], DFF + s * 256, 256, KC, half=1)
                    for j in range(2):
                        hc = s * 2 + j
                        bg = pbank()
                        proj_fm(g_sl, g_k, j * 128, 128, bg, hT, ["hT"])
                        bu = pbank()
                        proj_fm(u_sl, u_k, j * 128, 128, bu, hT, ["hT"])
                        fs = FS[hc % 2]
                        S.op("act", lambda e, bg=bg, fs=fs: e.activation(out=fs, in_=pb[bg][:], func=AF.Silu), r=[pbk(bg)], w=[("FS", hc % 2)])
                        S.op("dve", lambda e, bu=bu, fs=fs, hc=hc: e.tensor_tensor(out=HID[:, hc, :], in0=fs, in1=pb[bu][:], op=ALU.mult),
                             r=[pbk(bu), ("FS", hc % 2)], w=[("HID", hc)])
                for cg in range(8):
                    banks = [pbank(), pbank()]
                    for kh in range(2):
                        sl, slk = load_slab(wffd_d[l][kh * 2816:(kh + 1) * 2816, :], cg * 256, 256, 22)
                        for j in range(2):
                            def fn(e, sl=sl, j=j, kh=kh, bank=banks[j]):
                                ins = None
                                for kc in range(22):
                                    ins = e.matmul(pb[bank][:, :], lhsT=sl[:, kc, j * 128:(j + 1) * 128], rhs=HID[:, kh * 22 + kc, :],
                                                   start=(kh == 0 and kc == 0), stop=(kh == 1 and kc == 21))
                                return ins
                            S.op("pe", fn, r=list(slk) + [("HID", k) for k in range(kh * 22, kh * 22 + 22)], w=[pbk(banks[j])])
                    for j in range(2):
                        c = cg * 2 + j
                        S.op("dve", lambda e, c=c, bank=banks[j]: e.scalar_tensor_tensor(out=xT[:, c, :], in0=pb[bank][:], scalar=modT[:, l, 80 + c:81 + c],
                                                                                         in1=xT[:, c, :], op0=ALU.mult, op1=ALU.add),
                             r=[pbk(banks[j]), "modT", ("xT", c)], w=[("xT", c)])
                        ssq_chunk(c, "FS", defer=True)
                barrier()
                if tt == 0:
                    tap("x_l0_%d" % l, xT[:].rearrange("p c t -> p (c t)"), [128, KC * TT], [("xT", c) for c in range(KC)])

            for l_ in range(NL):
                layer_body(l_)
            if tt + 1 < NT:
                issue_xload(tt + 1)
            chk("ffn")
            ssq_rstd(have_ssq=True)
            for c in range(KC):
                S.op("dve", lambda e, c=c: e.scalar_tensor_tensor(out=xT[:, c, :], in0=xT[:, c, :], scalar=vT[:, V_FN + c:V_FN + c + 1],
                                                                  in1=RSTD, op0=ALU.mult, op1=ALU.mult),
                     r=[("xT", c), "RSTD", "vT"], w=[("xT", c)])
            for g in range(4):
                for cq in range(4):
                    bank = pbank()

                    def fn(e, g=g, cq=cq, bank=bank):
                        ins = None
                        for j in range(4):
                            c = cq * 4 + j
                            ins = e.transpose(out=pb[bank][:, j * 128:(j + 1) * 128], in_=xT[:, c, g * 128:(g + 1) * 128], identity=identf)
                        return ins
                    S.op("pe", fn, r=[("xT", cq * 4 + j) for j in range(4)] + ["consts"], w=[pbk(bank)])
                    copy_op(cpeng(), stage[:, g % 2, cq * 512:(cq + 1) * 512], pb[bank][:], [pbk(bank)], [("stage", g % 2)])
                S.dma("sp", ringS, out_d[t0 + g * 128:t0 + (g + 1) * 128, :], stage[:, g % 2, :], r=[("stage", g % 2)], w=[("out", tt, g)])
    except _Stop:
        pass
    S.op("sp", None, r=[("out", tt, g) for tt in range(NT) for g in range(4)] + [("tap", n) for n in taps])
    with nc.Block() as block:
        S.emit(block, esems)
    es.close()
    return nc, S


def pack_vecs(b, c, b_ada, norm1_w, norm2_w, gla_gate_b2, hgrn_lb, gla_norm_w, hgrn_norm_w, final_norm_w):
    v = np.zeros((V_ROWS, 128), np.float32)
    for l in range(2):
        base = l * V_LAYER
        v[base + V_BADA:base + V_BADA + 96] = b_ada[l].reshape(96, 128)
        v[base + V_N1:base + V_N1 + 16] = norm1_w[l].reshape(16, 128)
        v[base + V_N2:base + V_N2 + 16] = norm2_w[l].reshape(16, 128)
        v[base + V_B2:base + V_B2 + 2] = gla_gate_b2[l].reshape(2, 128)
        v[base + V_LB:base + V_LB + 4] = hgrn_lb[l].reshape(4, 128)
        v[base + V_GNW] = gla_norm_w[l]
        v[base + V_HNW] = hgrn_norm_w[l]
    v[V_FN:V_FN + 16] = final_norm_w.reshape(16, 128)
    v[V_C:V_C + 16] = c[b].reshape(16, 128)
    return v


_CACHE = {}


def make_in_maps(inputs, ncores=NCORES):
    f = lambda a: np.ascontiguousarray(np.asarray(a))
    x = f(inputs["x"])
    consts = make_consts()
    shared = dict(
        consts=consts, w_ada=f(inputs["w_ada"]), w_in=f(inputs["w_in"]), w2=f(inputs["gla_gate_w2"]),
        sinks=f(inputs["swa_sinks"]).reshape(1, 16), w_out=f(inputs["w_out"]),
        w_ffn_in=f(inputs["w_ffn_in"]), w_ffn_down=f(inputs["w_ffn_down"]))
    maps = []
    for b in range(ncores):
        m = dict(shared)
        m["x"] = x[b]
        m["pos"] = f(inputs["positions"])[b].reshape(1, SEQ).astype(np.int32)
        m["vecs"] = pack_vecs(b, f(inputs["c"]), f(inputs["b_ada"]), f(inputs["norm1_w"]), f(inputs["norm2_w"]),
                              f(inputs["gla_gate_b2"]), f(inputs["hgrn_lb"]), f(inputs["gla_norm_w"]),
                              f(inputs["hgrn_norm_w"]), f(inputs["final_norm_w"]))
        maps.append(m)
    return maps


def kernel(**inputs):
    if "nc" not in _CACHE:
        _CACHE["nc"] = build()[0]
    nc = _CACHE["nc"]
    maps = make_in_maps(inputs)
    res = run_bass_kernel_spmd(nc, maps, core_ids=list(range(NCORES)))
    return np.stack([np.asarray(r["out"]) for r in res.results], axis=0).astype(np.float32)
```

```python
import math
from contextlib import ExitStack
import numpy as np
import concourse.bass as bass
import concourse.mybir as mybir
from concourse.bass_utils import run_bass_kernel_spmd

F32 = mybir.dt.float32
BF16 = mybir.dt.bfloat16
I32 = mybir.dt.int32
AF = mybir.ActivationFunctionType
ALU = mybir.AluOpType
AX = mybir.AxisListType

SAME_ENGINE_RAW = True

D = 2048
KC = 16
SEQ = 4096
TT = 512
DFF = 5632
FC = 44
INW = 6416
EPS = 1e-6
NCORES = 4
TWO_PI = 2.0 * math.pi

C_IDENT = 0
C_ONES = 128
C_MASKA = 256
C_PERM_RET = 384
C_PERM_SWQ = 512
C_DUP = 640
C_SWAMASK = 1152
C_SWAMASK0 = 1408
C_RET_EQ = 1664
C_RET_EK = 1920
C_COLS = 2176
NCON = 2192

V_LAYER = 136
V_BADA, V_N1, V_N2, V_B2, V_LB, V_GNW, V_HNW = 0, 96, 112, 128, 130, 134, 135
V_FN = 272
V_C = 288
V_ROWS = 384


class Op:
    __slots__ = ("eng", "fn", "deps", "is_dma", "sem", "val", "signal", "epoch")

    def __init__(self, eng, fn, is_dma, epoch):
        self.eng = eng
        self.fn = fn
        self.is_dma = is_dma
        self.deps = []
        self.sem = None
        self.val = 0
        self.signal = is_dma
        self.epoch = epoch


class SemRing:
    def __init__(self, sems):
        self.sems = list(sems)
        self.cnt = [0] * len(self.sems)
        self.last = [None] * len(self.sems)
        self.i = 0


class Sched:
    ENGS = ("pe", "act", "dve", "pool", "sp")

    def __init__(self):
        self.ops = []
        self.last_w = {}
        self.readers = {}
        self.epoch = 0

    def _deps(self, op, r, w):
        deps = {}

        def add(y, kind):
            if y is None or y is op:
                return
            k = deps.get(id(y))
            if k is None:
                deps[id(y)] = [y, {kind}]
            else:
                k[1].add(kind)

        for k in r:
            add(self.last_w.get(k), "raw")
        for k in w:
            add(self.last_w.get(k), "waw")
            for rd in self.readers.get(k, ()):
                add(rd, "war")
        for y, kinds in deps.values():
            keep = False
            if y.is_dma or op.is_dma:
                keep = True
            elif y.eng != op.eng:
                keep = True
            elif op.eng != "pe" and SAME_ENGINE_RAW and "raw" in kinds:
                keep = True
            if keep:
                op.deps.append(y)
                y.signal = True
        for k in r:
            self.readers.setdefault(k, []).append(op)
        for k in w:
            self.last_w[k] = op
            self.readers[k] = []

    def op(self, eng, fn, r=(), w=()):
        o = Op(eng, fn, False, self.epoch)
        self._deps(o, r, w)
        self.ops.append(o)
        return o

    def dma(self, eng, ring, out, in_, r=(), w=()):
        o = Op(eng, (lambda e, out=out, in_=in_: e.dma_start(out=out, in_=in_)), True, self.epoch)
        self._deps(o, r, w)
        i = ring.i
        ring.i = (ring.i + 1) % len(ring.sems)
        prev = ring.last[i]
        if prev is not None and prev not in o.deps:
            o.deps.append(prev)
        ring.cnt[i] += 16
        o.sem = ring.sems[i]
        o.val = ring.cnt[i]
        ring.last[i] = o
        self.ops.append(o)
        return o

    def emit(self, block, eng_sems):
        cnt = {}
        for o in self.ops:
            if not o.is_dma and o.signal:
                key = (o.eng, o.epoch)
                cnt[key] = cnt.get(key, 0) + 1
                o.sem = eng_sems[o.eng][o.epoch]
                o.val = cnt[key]
        per = {e: [] for e in self.ENGS}
        for o in self.ops:
            per[o.eng].append(o)
        self.stats = {e: len(per[e]) for e in self.ENGS}
        self.maxcnt = max(cnt.values()) if cnt else 0

        def run(engname, e):
            waited = {}
            for o in per[engname]:
                need = {}
                for y in o.deps:
                    key = id(y.sem)
                    if key not in need or need[key][1] < y.val:
                        need[key] = (y.sem, y.val)
                for key, (sem, val) in need.items():
                    if waited.get(key, 0) >= val:
                        continue
                    e.wait_ge(sem, val)
                    waited[key] = val
                if o.fn is None:
                    continue
                ins = o.fn(e)
                if o.signal:
                    ins.then_inc(o.sem, 16 if o.is_dma else 1)

        @block.tensor
        def _(e):
            run("pe", e)

        @block.scalar
        def _(e):
            run("act", e)

        @block.vector
        def _(e):
            run("dve", e)

        @block.gpsimd
        def _(e):
            run("pool", e)

        @block.sync
        def _(e):
            run("sp", e)


def make_consts():
    c = np.zeros((128, NCON), np.float64)
    c[:, C_IDENT:C_IDENT + 128] = np.eye(128)
    c[:, C_ONES:C_ONES + 128] = 1.0
    j = np.arange(128)[:, None]
    i = np.arange(128)[None, :]
    c[:, C_MASKA:C_MASKA + 128] = ((j // 64 == i // 64) & (j <= i)).astype(np.float64)
    k = np.arange(128)[:, None]
    m = np.arange(128)[None, :]
    c[:, C_PERM_RET:C_PERM_RET + 128] = (k == (m + 64) % 128)

    def perm64(d):
        return np.where(d < 8, d + 8, np.where(d < 16, d - 8, d))

    c[:, C_PERM_SWQ:C_PERM_SWQ + 128] = (k == 64 * (m // 64) + perm64(m % 64))
    for kvh in range(2):
        c[:, C_DUP + (2 * kvh) * 128:C_DUP + (2 * kvh + 1) * 128] = (k == 64 * kvh + (m % 64))
        c[:, C_DUP + (2 * kvh + 1) * 128:C_DUP + (2 * kvh + 2) * 128] = (k == 64 * kvh + perm64(m % 64))
    q = np.arange(128)[:, None] + 128
    kp = np.arange(256)[None, :]
    rel = q - kp
    band = (rel >= 0) & (rel < 128)
    c[:, C_SWAMASK:C_SWAMASK + 256] = np.where(band, 0.0, -30000.0)
    c[:, C_SWAMASK0:C_SWAMASK0 + 256] = np.where(band & (kp >= 128), 0.0, -30000.0)
    jj = np.arange(64)
    for h in range(4):
        lg = math.log1p(-(2.0 ** (-5.0 - h)))
        c[:, C_RET_EQ + h * 64:C_RET_EQ + (h + 1) * 64] = np.exp(lg * (jj - 31))[None, :]
        c[:, C_RET_EK + h * 64:C_RET_EK + (h + 1) * 64] = np.exp(lg * (31 - jj))[None, :]
        c[:, C_COLS + 4 + 3 * h + 0] = math.exp(lg * 32)
        c[:, C_COLS + 4 + 3 * h + 1] = math.exp(lg * 64)
        c[:, C_COLS + 4 + 3 * h + 2] = math.exp(lg * 32)
    ret_inv = (1.0 / np.power(np.float32(10000.0), np.linspace(0.0, 1.0, 64, dtype=np.float32))).astype(np.float32)
    p = np.arange(128)
    c[:, C_COLS + 0] = ret_inv[p % 64]
    c[:, C_COLS + 1] = np.where(p < 64, -1.0, 1.0)
    rope_inv = (1.0 / np.power(np.float32(500000.0), np.arange(8, dtype=np.float32) / np.float32(8.0))).astype(np.float32)
    d = p % 64
    c[:, C_COLS + 2] = np.where(d < 16, rope_inv[d % 8], 0.0)
    c[:, C_COLS + 3] = np.where(d < 8, -1.0, np.where(d < 16, 1.0, 0.0))
    return c.astype(np.float32)


class _Stop(Exception):
    pass


def build(NT=8, NL=2, taps=None, stop=None):
    taps = taps or []

    def chk(name):
        if stop == name:
            raise _Stop()
    nc = bass.Bass("TRN2", target_bir_lowering=False)
    dram = lambda n, s, d, k="ExternalInput": nc.dram_tensor(n, s, d, kind=k).ap()
    x_d = dram("x", [SEQ, D], F32)
    pos_d = dram("pos", [1, SEQ], I32)
    vecs_d = dram("vecs", [V_ROWS, 128], F32)
    consts_d = dram("consts", [128, NCON], F32)
    wada_d = dram("w_ada", [2, D, 6 * D], F32)
    win_d = dram("w_in", [2, D, INW], F32)
    w2_d = dram("w2", [2, 16, 256], F32)
    sinks_d = dram("sinks", [1, 16], F32)
    wout_d = dram("w_out", [2, D, D], F32)
    wffi_d = dram("w_ffn_in", [2, D, 2 * DFF], F32)
    wffd_d = dram("w_ffn_down", [2, DFF, D], F32)
    out_d = dram("out", [SEQ, D], F32, "ExternalOutput")
    NSCR = 64
    wsc = nc.dram_tensor("wsc", [2, NSCR, 128, 8192], BF16).ap()
    tap_d = {}

    S = Sched()
    es = ExitStack()
    sb = lambda n, s, d: es.enter_context(nc.sbuf_tensor("s_" + n, s, d))
    mksem = lambda n: es.enter_context(nc.semaphore(n))

    consts = sb("consts", [128, NCON], F32)
    vstage = sb("vstage", [128, 3, 128], F32)
    vT = sb("vT", [128, V_ROWS], F32)
    identb = sb("identb", [128, 128], BF16)
    onesb = sb("onesb", [128, 128], BF16)
    condb = sb("condb", [128, 16], BF16)
    modT = sb("modT", [128, 2, 96], F32)
    nrm = sb("nrm", [128, 2, 2, 16], F32)
    smallc = sb("smallc", [128, 64], F32)
    sinkb = sb("sinkb", [128, 16], F32)
    w2b = sb("w2b", [16, 2, 256], BF16)
    xT = sb("xT", [128, KC, TT], F32)
    hT = sb("hT", [128, KC, TT], BF16)
    mixT = sb("mixT", [128, KC, TT], BF16)
    NSLAB = 2
    SLAB_E = 8192
    slab = sb("slab", [128, NSLAB, SLAB_E], BF16)
    stage = sb("stage", [128, 2, D], F32)
    rot = sb("rot", [128, 4, TT], F32)
    msk = sb("msk", [128, TT], F32)
    ST = sb("ST", [128, 2, 12, 128], F32)
    SKD = sb("SKD", [128, 2, 2, 640], BF16)
    SV = sb("SV", [128, 2, 5, 128], BF16)
    junk = sb("junk", [128, 8], F32)
    ARENA_B = 50688
    arena = sb("arena", [128, ARENA_B // 2], BF16)

    def carve(off, nbytes, dt):
        v = arena[:, off // 2:(off + nbytes) // 2]
        return v if dt == BF16 else v.bitcast(dt)

    EQ = carve(0, 8192, F32).rearrange("p (h t) -> p h t", t=TT)
    QT = carve(8192, 4096, BF16).rearrange("p (h t) -> p h t", t=TT)
    KT = carve(12288, 4096, BF16).rearrange("p (h t) -> p h t", t=TT)
    VTM = carve(16384, 4096, BF16).rearrange("p (g c) -> p g c", c=512)
    GATE = carve(20480, 4096, BF16).rearrange("p (h t) -> p h t", t=TT)
    T = [carve(24576 + 2048 * i, 2048, F32) for i in range(5)]
    rti = T[2].bitcast(I32)
    posi = T[3].bitcast(I32)
    posf = T[4]
    T12 = carve(24576, 4096, F32)
    T34b = carve(24576 + 4096, 4096, BF16)
    KTT = [carve(34816 + 1024 * i, 1024, BF16).rearrange("p (g c) -> p g c", c=128) for i in range(2)]
    ATb = [carve(36864 + 1024 * i, 1024, BF16) for i in range(2)]
    SBF = [carve(38912 + 2048 * i, 2048, BF16).rearrange("p (c v) -> p c v", v=128) for i in range(2)]
    XS = [carve(43008 + 512 * i, 512, F32) for i in range(2)]
    SQb = carve(44032, 2048, F32)
    RSTD = carve(46080, 2048, F32)
    TTb = carve(48128, 2048, F32)
    EC = carve(50176, 384, F32).rearrange("p (k h c) -> p k h c", k=3, h=4)
    HID = carve(0, 45056, BF16).rearrange("p (j t) -> p j t", t=TT)
    FS = [carve(45056 + 2048 * i, 2048, F32) for i in range(2)]
    EQK = [("EQ", 0), ("EQ", 1), ("EQ", 2)]
    arena_keys = (EQK + ["QT", "KT", "VTM", "GATE"] + [("T", i) for i in range(5)] + [("KTT", i) for i in range(2)]
                  + [("AT", i) for i in range(2)] + [("SBF", i) for i in range(2)] + [("XS", i) for i in range(2)]
                  + ["SQb", "RSTD", "TTb", "EC", ("FS", 0), ("FS", 1)] + [("HID", k) for k in range(FC)])

    pb = [es.enter_context(nc.psum_tensor("pb%d" % i, [128, 512], F32)) for i in range(8)]
    pbk = lambda i: ("pb", i)

    ringS = SemRing([mksem("rs%d" % i) for i in range(4)])
    ringW = SemRing([mksem("rw%d" % i) for i in range(4)])
    ringB = SemRing([mksem("rb%d" % i) for i in range(2)])
    NEPOCH = NT + 1
    esems = {e: [mksem("e_%s_%d" % (e, k)) for k in range(NEPOCH)] for e in ("pe", "act", "dve", "pool")}
    esems["sp"] = [mksem("e_sp")] * NEPOCH

    cc = lambda c0, n=128: consts[:, c0:c0 + n]
    ccol = lambda j: consts[:, C_COLS + j:C_COLS + j + 1]
    identf = cc(C_IDENT)
    onesf = cc(C_ONES)

    rr = {"proj": 0, "cp": 0, "slab": 0, "scratch": False, "layer": 0, "tt": 0, "sidx": -1}

    def pbank():
        b = rr["proj"]
        rr["proj"] = (b + 1) % 3
        return b

    def cpeng():
        rr["cp"] ^= 1
        return "act" if rr["cp"] else "dve"

    def copy_op(eng, out, in_, r, w):
        if eng == "act":
            S.op("act", lambda e: e.activation(out=out, in_=in_, func=AF.Copy), r=r, w=w)
        else:
            S.op(eng, lambda e: e.tensor_copy(out=out, in_=in_), r=r, w=w)

    def load_slab(w2d, c0, ncols, kcn, half=None):
        first = (half is None or half == 0)
        if first:
            s = rr["slab"]
            rr["slab"] = (s + 1) % NSLAB
            rr["cur"] = s
            rr["sidx"] += 1
        s = rr["cur"]
        idx, lyr = rr["sidx"], rr["layer"]
        off = 0 if not half else half * (SLAB_E // 2)
        view = slab[:, s, off:off + kcn * ncols].rearrange("p (k n) -> p k n", n=ncols)
        both = [("slab", s, 0), ("slab", s, 1)]
        wk = [("slab", s, half)] if half is not None else both
        if not rr["scratch"] or rr["tt"] <= lyr:
            src = w2d[:, c0:c0 + ncols].rearrange("(k p) n -> p k n", p=128)
            S.dma("pool", ringW, view, src, w=wk)
            if rr["scratch"] and rr["tt"] == lyr and (half is None or half == 1):
                S.dma("sp", ringB, wsc[lyr, idx], slab[:, s, :], r=both, w=[("wsc", lyr, idx)])
        elif first:
            n = SLAB_E if half is not None else kcn * ncols
            S.dma("pool", ringW, slab[:, s, 0:n], wsc[lyr, idx][:, 0:n], r=[("wsc", lyr, idx)], w=both)
        return view, wk

    def proj_fm(sl, slk, j0, M, bank, rhsT, rkeys, kcn=KC):
        def fn(e):
            ins = None
            for kc in range(kcn):
                ins = e.matmul(pb[bank][0:M, :], lhsT=sl[:, kc, j0:j0 + M], rhs=rhsT[:, kc, :],
                               start=(kc == 0), stop=(kc == kcn - 1))
            return ins
        S.op("pe", fn, r=list(slk) + list(rkeys), w=[pbk(bank)])

    def proj_tm(sl, slk, n0, ncols, tg, bank):
        def fn(e):
            ins = None
            for kc in range(KC):
                ins = e.matmul(pb[bank][:, 0:ncols], lhsT=hT[:, kc, tg * 128:(tg + 1) * 128],
                               rhs=sl[:, kc, n0:n0 + ncols], start=(kc == 0), stop=(kc == KC - 1))
            return ins
        S.op("pe", fn, r=list(slk) + ["hT"], w=[pbk(bank)])

    def tap(name, ap, shape, keys):
        if name not in taps:
            return
        t = dram("tap_" + name, list(shape), F32, "ExternalOutput")
        tap_d[name] = t
        if ap.dtype == F32:
            S.dma("sp", ringS, t, ap, r=keys, w=[("tap", name)])
        else:
            S.dma("pool", ringW, t, ap, r=keys, w=[("tap", name)])

    ssq_pending = []

    def ssq_flush():
        while ssq_pending:
            ssq_pending.pop(0)()

    def barrier():
        ssq_flush()
        S.op("dve", lambda e: e.memset(junk[:, 0:1], 0.0), w=arena_keys)

    S.dma("sp", ringS, consts[:], consts_d, w=["consts"])
    S.dma("sp", ringS, vstage[:], vecs_d.rearrange("(g r) c -> r g c", g=3), w=["vstage"])
    S.dma("sp", ringS, sinkb[:], bass.AP(sinks_d.tensor, 0, [[0, 128], [1, 16]]), w=["sinkb"])
    S.dma("pool", ringW, w2b[:], w2_d.rearrange("l k n -> k l n"), w=["w2b"])

    def fn(e):
        ins = None
        for g in range(3):
            ins = e.transpose(out=pb[0][:, g * 128:(g + 1) * 128], in_=vstage[:, g, :], identity=identf)
        return ins
    S.op("pe", fn, r=["consts", "vstage"], w=[pbk(0)])
    S.op("dve", lambda e: e.tensor_copy(out=vT[:], in_=pb[0][:, 0:V_ROWS]), r=[pbk(0)], w=["vT"])
    S.op("dve", lambda e: e.tensor_copy(out=identb[:], in_=identf), r=["consts"], w=["identb"])
    S.op("dve", lambda e: e.tensor_copy(out=onesb[:], in_=onesf), r=["consts"], w=["onesb"])
    S.op("act", lambda e: e.activation(out=condb[:], in_=vT[:, V_C:V_C + 16], func=AF.Silu), r=["vT"], w=["condb"])
    S.op("pool", lambda e: e.memset(msk[:], 1.0), w=["msk"])
    S.op("pool", lambda e: e.memset(msk[:].rearrange("p (c j) -> p c j", j=64)[:, :, 0:1], 0.0), r=["msk"], w=["msk"])
    S.op("pool", lambda e: e.memset(ST[:], 0.0), w=["ST"])
    S.op("pool", lambda e: e.memset(SKD[:], 0.0), w=["SKD"])
    S.op("pool", lambda e: e.memset(SV[:], 0.0), w=["SV"])
    LB0, OML0, NB2 = 0, 8, 16
    S.op("dve", lambda e: e.memset(smallc[:, 0:4], 0.0), w=["smallc"])
    S.op("dve", lambda e: e.tensor_tensor(out=smallc[:, 32:36], in0=vT[:, V_LB:V_LB + 4],
                                          in1=vT[:, V_LAYER + V_LB:V_LAYER + V_LB + 4], op=ALU.subtract), r=["vT"], w=["smallc_t"])
    S.op("act", lambda e: e.activation(out=smallc[:, 36:40], in_=smallc[:, 32:36], func=AF.Exp), r=["smallc_t"], w=["smallc_t2"])
    S.op("dve", lambda e: e.tensor_scalar(out=smallc[:, 40:44], in0=smallc[:, 36:40], scalar1=1.0, scalar2=None, op0=ALU.add),
         r=["smallc_t2"], w=["smallc_t3"])
    S.op("dve", lambda e: e.reciprocal(out=smallc[:, 4:8], in_=smallc[:, 40:44]), r=["smallc_t3", "smallc"], w=["smallc"])
    S.op("dve", lambda e: e.tensor_scalar(out=smallc[:, 8:16], in0=smallc[:, 0:8], scalar1=-1.0, scalar2=1.0,
                                          op0=ALU.mult, op1=ALU.add), r=["smallc"], w=["smallc"])
    for l in range(2):
        S.op("dve", lambda e, l=l: e.tensor_scalar(out=smallc[:, NB2 + 2 * l:NB2 + 2 * l + 2],
                                                   in0=vT[:, l * V_LAYER + V_B2:l * V_LAYER + V_B2 + 2],
                                                   scalar1=-1.0, scalar2=None, op0=ALU.mult), r=["vT", "smallc"], w=["smallc"])

    for l in range(NL):
        for s in range(24):
            sl, slk = load_slab(wada_d[l], s * 512, 512, KC)

            def fn(e, sl=sl, s=s):
                ins = None
                for j in range(4):
                    col = s * 4 + j
                    for kc in range(KC):
                        ins = e.matmul(pb[3][:, col:col + 1], lhsT=sl[:, kc, j * 128:(j + 1) * 128],
                                       rhs=condb[:, kc:kc + 1], start=(kc == 0), stop=(kc == KC - 1))
                return ins
            S.op("pe", fn, r=slk + ["condb"], w=[pbk(3)])
        S.op("dve", lambda e, l=l: e.tensor_tensor(out=modT[:, l, :], in0=pb[3][:, 0:96],
                                                   in1=vT[:, l * V_LAYER:l * V_LAYER + 96], op=ALU.add),
             r=[pbk(3), "vT"], w=["modT"])
        for k, (nrow, sc0) in enumerate(((V_N1, 16), (V_N2, 64))):
            S.op("dve", lambda e, l=l, k=k, nrow=nrow, sc0=sc0: e.scalar_tensor_tensor(
                out=nrm[:, l, k, :], in0=modT[:, l, sc0:sc0 + 16], scalar=1.0,
                in1=vT[:, l * V_LAYER + nrow:l * V_LAYER + nrow + 16], op0=ALU.add, op1=ALU.mult),
                r=["modT", "vT"], w=["nrm"])
    tap("modT", modT[:].rearrange("p l j -> p (l j)"), [128, 192], ["modT"])

    def ssq_chunk(c, which, defer=False):
        if which == "T":
            sq, key = T[c % 2].bitcast(BF16)[:, 0:TT], ("T", c % 2)
        else:
            sq, key = FS[c % 2].bitcast(BF16)[:, 0:TT], ("FS", c % 2)
        prev = list(ssq_pending)
        del ssq_pending[:]
        if c % 2 == 0:
            S.op("act", lambda e: e.activation(out=sq, in_=xT[:, c, :], func=AF.Square), r=[("xT", c)], w=[key])
        else:
            S.op("dve", lambda e: e.tensor_tensor(out=sq, in0=xT[:, c, :], in1=xT[:, c, :], op=ALU.mult), r=[("xT", c)], w=[key])

        def mm():
            S.op("pe", lambda e: e.matmul(pb[7][:], lhsT=onesb[:], rhs=sq, start=(c == 0), stop=(c == KC - 1)),
                 r=[key, "onesb"], w=[pbk(7)])
        for p in prev:
            p()
        if defer:
            ssq_pending.append(mm)
        else:
            mm()

    def rstd_finish():
        ssq_flush()
        S.op("act", lambda e: e.activation(out=RSTD, in_=pb[7][:], func=AF.Ln, scale=1.0 / D, bias=EPS), r=[pbk(7)], w=["RSTD"])
        S.op("act", lambda e: e.activation(out=RSTD, in_=RSTD, func=AF.Exp, scale=-0.5), r=["RSTD"], w=["RSTD"])

    def ssq_rstd(have_ssq=False):
        if not have_ssq:
            for c in range(KC):
                ssq_chunk(c, "T")
        rstd_finish()

    def norm_to_hT(l, k, shift0, have_ssq=False):
        ssq_rstd(have_ssq)
        for c in range(KC):
            tmp = T[2 + c % 2]
            S.op("dve", lambda e, c=c, tmp=tmp: e.tensor_tensor(out=tmp, in0=xT[:, c, :], in1=RSTD, op=ALU.mult),
                 r=[("xT", c), "RSTD"], w=[("T", 2 + c % 2)])
            S.op("act", lambda e, c=c, tmp=tmp: e.activation(out=hT[:, c, :], in_=tmp, func=AF.Identity,
                                                             scale=nrm[:, l, k, c:c + 1],
                                                             bias=modT[:, l, shift0 + c:shift0 + c + 1]),
                 r=[("T", 2 + c % 2), "nrm", "modT"], w=["hT"])

    XSA = carve(24576, 4096, F32).rearrange("p (c v) -> p c v", v=128)
    Hh = carve(24576 + 4096, 6144, F32)[:, 0:9 * 128].rearrange("p (c v) -> p c v", v=128)
    XK = [("T", 0), ("T", 1)]
    HK = [("T", 2), ("T", 3), ("T", 4)]

    def evec_ec(r0, r1, kind, hslot):
        return EC[r0:r1, kind, hslot, :]

    def evec_const(col):
        return bass.AP(consts[:].tensor, C_COLS + col, [[consts[:].ap[0][0], 128], [0, 8]])

    def bc_half(vec, half):
        st = vec.ap[1][0]
        return bass.AP(vec.tensor, vec.offset + half * st, [[vec.ap[0][0], vec.ap[0][1]], [2 * st, 4], [0, 128]])

    def bc_all(vec):
        st = vec.ap[1][0]
        return bass.AP(vec.tensor, vec.offset, [[vec.ap[0][0], vec.ap[0][1]], [st, 8], [0, 128]])

    def head_A(l, buf, kbuf, kt_tile, q_tile, rows, v_cols, st_head, evecs, do_kT=True):
        r0, r1 = rows
        e_mid, e_last, e_lm = evecs
        ktt = KTT[kbuf]
        if do_kT:
            def fn(e):
                ins = None
                pv = pb[7][:].bitcast(BF16)
                for tg in range(4):
                    ins = e.transpose(out=pv[:, tg * 128:(tg + 1) * 128], in_=kt_tile[:, tg * 128:(tg + 1) * 128], identity=identb[:])
                return ins
            S.op("pe", fn, r=["KT", "identb"], w=[pbk(7)])
            copy_op("act", ktt.rearrange("p g c -> p (g c)"), pb[7][:].bitcast(BF16)[:, 0:512], [pbk(7)], [("KTT", kbuf)])
        def fn(e):
            ins = None
            for tg in range(4):
                ins = e.matmul(pb[3][:, tg * 128:(tg + 1) * 128], lhsT=kt_tile[r0:r1, tg * 128:(tg + 1) * 128],
                               rhs=q_tile[r0:r1, tg * 128:(tg + 1) * 128], start=True, stop=True)
            return ins
        S.op("pe", fn, r=["KT", "QT"], w=[pbk(3)])
        at = ATb[buf]
        mI = cc(C_MASKA).bitcast(I32)
        S.op("dve", lambda e: e.copy_predicated(out=at.rearrange("p (g t) -> p g t", t=128),
                                                mask=bass.AP(mI.tensor, mI.offset, [[mI.ap[0][0], 128], [0, 4], [1, 128]]),
                                                data=pb[3][:].rearrange("p (g t) -> p g t", t=128)),
             r=[pbk(3), "consts", ("AT", buf)], w=[("AT", buf)])
        def fn(e):
            ins = None
            for c in range(8):
                tg, half = c // 2, c % 2
                ins = e.matmul(pb[4 + half][r0:r1, tg * 128:(tg + 1) * 128],
                               lhsT=ktt[half * 64:(half + 1) * 64, tg, r0:r1],
                               rhs=VTM[half * 64:(half + 1) * 64, tg, v_cols[0]:v_cols[1]], start=True, stop=True)
            return ins
        S.op("pe", fn, r=[("KTT", kbuf), "VTM"], w=[pbk(4), pbk(5)])
        sbf = SBF[buf]
        Sv = ST[r0:r1, l, st_head, :]
        stk = ("ST", l, st_head)
        S.op("act", lambda e: e.activation(out=Hh[r0:r1, 0, :], in_=Sv, func=AF.Copy), r=[stk], w=HK)
        xs4 = XSA.rearrange("p (g h) v -> p g h v", h=2)
        for half in range(2):
            S.op("dve", lambda e, half=half: e.tensor_tensor(out=xs4[r0:r1, :, half, :],
                                                             in0=pb[4 + half][r0:r1, :].rearrange("p (g v) -> p g v", v=128),
                                                             in1=bc_half(e_lm, half), op=ALU.mult),
                 r=[pbk(4 + half), "EC"], w=XK)
        for c in range(8):
            S.op("dve", lambda e, c=c: e.scalar_tensor_tensor(out=Hh[r0:r1, c + 1, :], in0=Hh[r0:r1, c, :], scalar=e_last[:, c:c + 1],
                                                              in1=XSA[r0:r1, c, :], op0=ALU.mult, op1=ALU.add),
                 r=HK + XK + ["EC"], w=HK)
        S.op("dve", lambda e: e.tensor_tensor(out=sbf[r0:r1, :, :], in0=Hh[r0:r1, 0:8, :], in1=bc_all(e_mid), op=ALU.mult),
             r=HK + ["EC"], w=[("SBF", buf)])
        S.op("act", lambda e: e.activation(out=Sv, in_=Hh[r0:r1, 8, :], func=AF.Copy), r=HK, w=[stk])

    def head_C(l, buf, q_tile, rows, v_cols, nw_col, gate_idx, mix_chunk, part=0):
      r0, r1 = rows
      at = ATb[buf]
      sbf = SBF[buf]
      if part in (0, 1):
        def fn(e):
            ins = None
            for tg in range(4):
                e.matmul(pb[6][:, tg * 128:(tg + 1) * 128], lhsT=VTM[:, tg, v_cols[0]:v_cols[1]],
                         rhs=at[:, tg * 128:(tg + 1) * 128], start=True, stop=False)
                for half in range(2):
                    c = 2 * tg + half
                    ins = e.matmul(pb[6][:, c * 64:(c + 1) * 64], lhsT=sbf[r0:r1, c, :], rhs=q_tile[r0:r1, c * 64:(c + 1) * 64],
                                   start=False, stop=(half == 1))
            return ins
        S.op("pe", fn, r=["VTM", ("AT", buf), ("SBF", buf), "QT"], w=[pbk(6)])
        sqb = SQb.bitcast(BF16)[:, 0:TT]
        S.op("act", lambda e: e.activation(out=sqb, in_=pb[6][:], func=AF.Square), r=[pbk(6)], w=["SQb"])
        if part == 1:
            return
      if True:
        sqb = SQb.bitcast(BF16)[:, 0:TT]
        S.op("pe", lambda e: e.matmul(pb[7][:], lhsT=onesb[:], rhs=sqb, start=True, stop=True), r=["SQb", "onesb"], w=[pbk(7)])
        S.op("act", lambda e: e.activation(out=RSTD, in_=pb[7][:], func=AF.Ln, scale=1.0 / 128, bias=EPS), r=[pbk(7)], w=["RSTD"])
        S.op("act", lambda e: e.activation(out=RSTD, in_=RSTD, func=AF.Exp, scale=-0.5), r=["RSTD"], w=["RSTD"])
        S.op("dve", lambda e: e.scalar_tensor_tensor(out=TTb, in0=pb[6][:], scalar=(nw_col if nw_col is not None else 1.0),
                                                     in1=RSTD, op0=ALU.mult, op1=ALU.mult),
             r=[pbk(6), "RSTD", "vT"], w=["TTb"])
        S.op("dve", lambda e: e.tensor_tensor(out=mixT[:, mix_chunk, :], in0=TTb, in1=GATE[:, gate_idx, :], op=ALU.mult),
             r=["TTb", "GATE"], w=[("mixT", mix_chunk)])

    def run_fill(fillers, n):
        for _ in range(n):
            if fillers:
                fillers.pop(0)()

    def run_heads(specs, fillers=None):
        fillers = fillers if fillers is not None else []
        n = len(specs)
        A = lambda i: head_A(*specs[i][0])
        C1 = lambda i: head_C(*specs[i][1], part=1)
        C2 = lambda i: head_C(*specs[i][1], part=2)
        A(0)
        run_fill(fillers, 1)
        for i in range(n):
            if i + 1 < n:
                A(i + 1)
                run_fill(fillers, 1)
            if i > 0:
                C2(i - 1)
                run_fill(fillers, 1)
            C1(i)
            run_fill(fillers, 1)
        run_fill(fillers, 2)
        C2(n - 1)
        run_fill(fillers, len(fillers))

    def decay_tables(bsrc_tile, key_b, scale_q, hslot, tD, tEk, keyD, keyEk):
        b3 = bsrc_tile.rearrange("p (c j) -> p c j", j=64)
        d3 = tD.rearrange("p (c j) -> p c j", j=64)
        S.op("dve", lambda e: e.tensor_tensor(out=d3, in0=b3, in1=b3[:, :, 31:32].to_broadcast([128, 8, 64]), op=ALU.subtract),
             r=[key_b], w=[keyD])
        S.op("act", lambda e: e.activation(out=EQ[:, hslot, :], in_=tD, func=AF.Exp, scale=scale_q), r=[keyD], w=EQK)
        S.op("act", lambda e: e.activation(out=tEk, in_=tD, func=AF.Exp, scale=-scale_q), r=[keyD], w=[keyEk])
        S.op("act", lambda e: e.activation(out=EC[:, 0, hslot, :], in_=b3[:, :, 31], func=AF.Exp, scale=scale_q), r=[key_b], w=["EC"])
        S.op("act", lambda e: e.activation(out=EC[:, 1, hslot, :], in_=b3[:, :, 63], func=AF.Exp, scale=scale_q), r=[key_b], w=["EC"])
        S.op("act", lambda e: e.activation(out=EC[:, 2, hslot, :], in_=d3[:, :, 63], func=AF.Exp, scale=scale_q), r=[keyD], w=["EC"])

    def rotate_from_psum(bank, perm_c0, cos_t, sin_t, out_fn):
        S.op("act", lambda e: e.activation(out=T[0], in_=pb[bank][:], func=AF.Copy), r=[pbk(bank)], w=[("T", 0)])
        S.op("pe", lambda e: e.matmul(pb[7][:], lhsT=cc(perm_c0), rhs=T[0], start=True, stop=True),
             r=[("T", 0), "consts"], w=[pbk(7)])
        S.op("dve", lambda e: e.tensor_tensor(out=T[1], in0=T[0], in1=cos_t, op=ALU.mult), r=[("T", 0), "rot"], w=[("T", 1)])
        S.op("dve", lambda e: e.tensor_tensor(out=T[2], in0=pb[7][:], in1=sin_t, op=ALU.mult), r=[pbk(7), "rot"], w=[("T", 2)])
        S.op("dve", lambda e: e.tensor_tensor(out=T[1], in0=T[1], in1=T[2], op=ALU.add), r=[("T", 1), ("T", 2)], w=[("T", 1)])
        out_fn(T[1])

    cosR, sinR, cosS, sinS = rot[:, 0, :], rot[:, 1, :], rot[:, 2, :], rot[:, 3, :]

    def bc_chunks(c0):
        return bass.AP(consts[:].tensor, c0, [[consts[:].ap[0][0], 128], [0, 8], [1, 64]])

    XST = carve(0, 32768, F32).rearrange("p (g n) -> p g n", n=D)
    XSTK = [list(EQK), ["QT", "KT"], ["VTM", "GATE"], [("T", i) for i in range(4)]]

    def issue_xload(tile):
        for g in range(4):
            S.dma("sp", ringS, XST[:, g, :], x_d[tile * TT + g * 128:tile * TT + (g + 1) * 128, :], w=XSTK[g])

    try:
        chk("setup")
        for tt in range(NT):
            S.epoch = tt + 1
            t0 = tt * TT
            if tt == 0:
                issue_xload(0)
            for g in range(4):
                for cq in range(4):
                    bank = pbank()

                    def fn(e, g=g, cq=cq, bank=bank):
                        ins = None
                        for j in range(4):
                            c = cq * 4 + j
                            ins = e.transpose(out=pb[bank][:, j * 128:(j + 1) * 128], in_=XST[:, g, c * 128:(c + 1) * 128], identity=identf)
                        return ins
                    S.op("pe", fn, r=XSTK[g] + ["consts"], w=[pbk(bank)])
                    copy_op(cpeng(), xT[:, cq * 4:cq * 4 + 4, g * 128:(g + 1) * 128],
                            pb[bank][:].rearrange("p (j t) -> p j t", t=128), [pbk(bank)], [("xT", cq * 4 + j) for j in range(4)])
            chk("xload")
            S.dma("sp", ringS, posi, bass.AP(pos_d.tensor, t0, [[0, 128], [1, TT]]), w=[("T", 3)])
            S.op("dve", lambda e: e.tensor_copy(out=posf, in_=posi), r=[("T", 3)], w=[("T", 4)])
            for ti, (icol, scol) in enumerate(((0, 1), (2, 3))):
                for which in range(2):
                    dst = rot[:, ti * 2 + (1 - which), :]
                    add = 0.0 if which == 0 else math.pi / 2
                    S.op("dve", lambda e, icol=icol, add=add: e.tensor_scalar(out=T[0], in0=posf, scalar1=ccol(icol), scalar2=add,
                                                                              op0=ALU.mult, op1=ALU.add), r=[("T", 4), "consts"], w=[("T", 0)])
                    S.op("dve", lambda e: e.tensor_scalar(out=rti, in0=T[0], scalar1=1.0 / TWO_PI, scalar2=None, op0=ALU.mult),
                         r=[("T", 0)], w=[("T", 2)])
                    S.op("dve", lambda e: e.tensor_copy(out=T[1], in_=rti), r=[("T", 2)], w=[("T", 1)])
                    S.op("dve", lambda e: e.scalar_tensor_tensor(out=T[1], in0=T[1], scalar=-TWO_PI, in1=T[0],
                                                                 op0=ALU.mult, op1=ALU.add), r=[("T", 0), ("T", 1)], w=[("T", 1)])
                    if which == 0:
                        S.op("act", lambda e, dst=dst, scol=scol: e.activation(out=dst, in_=T[1], func=AF.Sin, scale=ccol(scol)),
                             r=[("T", 1), "consts"], w=["rot"])
                    else:
                        S.op("act", lambda e, dst=dst: e.activation(out=dst, in_=T[1], func=AF.Sin), r=[("T", 1)], w=["rot"])
            chk("rot")
            if tt == 0:
                tap("rot", rot[:].rearrange("p a t -> p (a t)"), [128, 4 * TT], ["rot"])
                tap("xT", xT[:].rearrange("p c t -> p (c t)"), [128, KC * TT], [("xT", c) for c in range(KC)])

            def layer_body(l):
                vb = l * V_LAYER
                rr["scratch"], rr["layer"], rr["tt"], rr["sidx"] = True, l, tt, -1
                for i in range(2):
                    S.op("pool", lambda e, i=i: e.memset(ATb[i], 0.0), w=[("AT", i)])
                norm_to_hT(l, 0, 0, have_ssq=(l > 0))
                if tt == 0:
                    tap("h1_%d" % l, hT[:].rearrange("p c t -> p (c t)"), [128, KC * TT], ["hT"])
                win = win_d[l]
                chk("norm1")

                wo_buf = stage[:].rearrange("p a n -> p (a n)").bitcast(BF16).rearrange("p (k n) -> p k n", n=D)
                WOK = [("stage", 0), ("stage", 1)]

                def outproj_partial(g, last):
                    wo_flat = stage[:].rearrange("p a n -> p (a n)").bitcast(BF16)
                    if tt <= l:
                        S.dma("pool", ringW, wo_buf, wout_d[l][g * 512:(g + 1) * 512, :].rearrange("(k p) n -> p k n", p=128), w=WOK)
                        if tt == l:
                            S.dma("sp", ringB, wsc[l, 60 + g], wo_flat, r=WOK, w=[("wsc", l, 60 + g)])
                    else:
                        S.dma("pool", ringW, wo_flat, wsc[l, 60 + g], r=[("wsc", l, 60 + g)], w=WOK)
                    fl = []
                    for c in range(KC):
                        def f(c=c):
                            bank = pbank()

                            def fn(e):
                                ins = None
                                for kc in range(4):
                                    ins = e.matmul(pb[bank][:, :], lhsT=wo_buf[:, kc, c * 128:(c + 1) * 128], rhs=mixT[:, 4 * g + kc, :],
                                                   start=(kc == 0), stop=(kc == 3))
                                return ins
                            S.op("pe", fn, r=WOK + [("mixT", 4 * g + k) for k in range(4)], w=[pbk(bank)])
                            S.op("dve", lambda e: e.scalar_tensor_tensor(out=xT[:, c, :], in0=pb[bank][:], scalar=modT[:, l, 32 + c:33 + c],
                                                                         in1=xT[:, c, :], op0=ALU.mult, op1=ALU.add),
                                 r=[pbk(bank), "modT", ("xT", c)], w=[("xT", c)])
                            if last:
                                ssq_chunk(c, "T", defer=True)
                        fl.append(f)
                    return fl

                ar_sl, ar_k = load_slab(win, 1536, 16, KC)
                bank = pbank()
                proj_fm(ar_sl, ar_k, 0, 16, bank, hT, ["hT"])
                arT = TTb.bitcast(BF16)[0:16, 0:TT]
                S.op("dve", lambda e, bank=bank: e.tensor_copy(out=arT, in_=pb[bank][0:16, :]), r=[pbk(bank)], w=["TTb"])
                qk_sl, qk_k = load_slab(win, 0, 512, KC)
                for pt in range(2):
                    bank = pbank()
                    S.op("pe", lambda e, pt=pt, bank=bank: e.matmul(pb[bank][:], lhsT=w2b[0:16, l, pt * 128:(pt + 1) * 128], rhs=arT,
                                                                    start=True, stop=True), r=["w2b", "TTb"], w=[pbk(bank)])
                    S.op("act", lambda e, pt=pt, bank=bank: e.activation(out=T[0], in_=pb[bank][:], func=AF.Exp, scale=-1.0,
                                                                          bias=smallc[:, NB2 + 2 * l + pt:NB2 + 2 * l + pt + 1]),
                         r=[pbk(bank), "smallc"], w=[("T", 0)])
                    S.op("act", lambda e: e.activation(out=T[0], in_=T[0], func=AF.Ln, bias=1.0), r=[("T", 0)], w=[("T", 0)])
                    S.op("dve", lambda e: e.tensor_tensor_scan(out=T[1], data0=msk[:], data1=T[0], initial=0.0, op0=ALU.mult, op1=ALU.add),
                         r=[("T", 0), "msk"], w=[("T", 1)])
                    decay_tables(T[1], ("T", 1), -1.0 / 16.0, pt, T[2], T[3], ("T", 2), ("T", 3))
                    bank = pbank()
                    proj_fm(qk_sl, qk_k, pt * 128, 128, bank, hT, ["hT"])
                    S.op("dve", lambda e, pt=pt, bank=bank: e.scalar_tensor_tensor(out=QT[:, pt, :], in0=pb[bank][:], scalar=0.125,
                                                                                   in1=EQ[:, pt, :], op0=ALU.mult, op1=ALU.mult),
                         r=[pbk(bank)] + EQK, w=["QT"])
                    bank = pbank()
                    proj_fm(qk_sl, qk_k, 256 + pt * 128, 128, bank, hT, ["hT"])
                    S.op("dve", lambda e, pt=pt, bank=bank: e.tensor_tensor(out=KT[:, pt, :], in0=pb[bank][:], in1=T[3], op=ALU.mult),
                         r=[pbk(bank), ("T", 3)], w=["KT"])
                if tt == 0:
                    tap("gla_q_%d" % l, QT[:, 0:2, :].rearrange("p h t -> p (h t)"), [128, 2 * TT], ["QT"])
                    tap("gla_k_%d" % l, KT[:, 0:2, :].rearrange("p h t -> p (h t)"), [128, 2 * TT], ["KT"])
                    tap("gla_ec_%d" % l, EC.rearrange("p k h c -> p (k h c)"), [128, 96], ["EC"])

                chk("gla_proj")

                def v_and_gate(c_v, c_g):
                    v_sl, v_k = load_slab(win, c_v, 512, KC)
                    for tg in range(4):
                        bank = pbank()
                        proj_tm(v_sl, v_k, 0, 512, tg, bank)
                        copy_op("act", VTM[:, tg, :], pb[bank][:], [pbk(bank)], ["VTM"])
                    g_sl, g_k = load_slab(win, c_g, 512, KC)
                    for j in range(4):
                        bank = pbank()
                        proj_fm(g_sl, g_k, j * 128, 128, bank, hT, ["hT"])
                        S.op("act", lambda e, j=j, bank=bank: e.activation(out=GATE[:, j, :], in_=pb[bank][:], func=AF.Silu),
                             r=[pbk(bank)], w=["GATE"])

                v_and_gate(512, 1024)
                specs = []
                for h in range(4):
                    pt, hp = h // 2, h % 2
                    rows = (hp * 64, hp * 64 + 64)
                    vc = (h * 128, h * 128 + 128)
                    ev = tuple(evec_ec(rows[0], rows[1], kind, pt) for kind in range(3))
                    specs.append(((l, h % 2, pt, KT[:, pt, :], QT[:, pt, :], rows, vc, h, ev, hp == 0),
                                  (l, h % 2, QT[:, pt, :], rows, vc, vT[:, vb + V_GNW:vb + V_GNW + 1], h, h)))
                run_heads(specs)
                if tt == 0:
                    tap("mix_gla_%d" % l, mixT[:, 0:4, :].rearrange("p c t -> p (c t)"), [128, 4 * TT], [("mixT", c) for c in range(4)])

                chk("gla")
                for which, c0 in ((0, 1552), (1, 2064)):
                    sl, slk = load_slab(win, c0, 512, KC)
                    for h in range(4):
                        bank = pbank()
                        proj_fm(sl, slk, h * 128, 128, bank, hT, ["hT"])
                        if which == 0:
                            rotate_from_psum(bank, C_PERM_RET, cosR, sinR, lambda t, h=h: S.op(
                                "dve", lambda e: e.scalar_tensor_tensor(out=QT[:, h, :].rearrange("p (c j) -> p c j", j=64),
                                                                        in0=t.rearrange("p (c j) -> p c j", j=64), scalar=128.0 ** -0.5,
                                                                        in1=bc_chunks(C_RET_EQ + h * 64), op0=ALU.mult, op1=ALU.mult),
                                r=[("T", 1), "consts"], w=["QT"]))
                        else:
                            rotate_from_psum(bank, C_PERM_RET, cosR, sinR, lambda t, h=h: S.op(
                                "dve", lambda e: e.tensor_tensor(out=KT[:, h, :].rearrange("p (c j) -> p c j", j=64),
                                                                 in0=t.rearrange("p (c j) -> p c j", j=64),
                                                                 in1=bc_chunks(C_RET_EK + h * 64), op=ALU.mult),
                                r=[("T", 1), "consts"], w=["KT"]))
                v_and_gate(2576, 3088)
                specs = []
                for h in range(4):
                    vc = (h * 128, h * 128 + 128)
                    ev = tuple(evec_const(4 + 3 * h + kind) for kind in range(3))
                    specs.append(((l, h % 2, h % 2, KT[:, h, :], QT[:, h, :], (0, 128), vc, 4 + h, ev, True),
                                  (l, h % 2, QT[:, h, :], (0, 128), vc, None, h, 4 + h)))
                fill = outproj_partial(0, False)
                run_heads(specs, fill)
                if tt == 0:
                    tap("mix_ret_%d" % l, mixT[:, 4:8, :].rearrange("p c t -> p (c t)"), [128, 4 * TT], [("mixT", c) for c in range(4, 8)])

                chk("ret")
                sl, slk = load_slab(win, 3600, 512, KC)
                for j in range(4):
                    bank = pbank()
                    proj_fm(sl, slk, j * 128, 128, bank, hT, ["hT"])
                    rotate_from_psum(bank, C_PERM_SWQ, cosS, sinS, lambda t, j=j: S.op(
                        "act", lambda e: e.activation(out=QT[:, j, :], in_=t, func=AF.Copy), r=[("T", 1)], w=["QT"]))
                sl, slk = load_slab(win, 4112, 256, KC)
                bank = pbank()
                proj_fm(sl, slk, 0, 128, bank, hT, ["hT"])
                S.op("act", lambda e, bank=bank: e.activation(out=T[0], in_=pb[bank][:], func=AF.Copy), r=[pbk(bank)], w=[("T", 0)])
                for kvh in range(2):
                    S.op("pe", lambda e, kvh=kvh: e.matmul(pb[6][:], lhsT=cc(C_DUP + (2 * kvh) * 128), rhs=T[0], start=True, stop=True),
                         r=[("T", 0), "consts"], w=[pbk(6)])
                    S.op("pe", lambda e, kvh=kvh: e.matmul(pb[7][:], lhsT=cc(C_DUP + (2 * kvh + 1) * 128), rhs=T[0], start=True, stop=True),
                         r=[("T", 0), "consts"], w=[pbk(7)])
                    S.op("dve", lambda e: e.tensor_tensor(out=T[1], in0=pb[6][:], in1=cosS, op=ALU.mult), r=[pbk(6), "rot"], w=[("T", 1)])
                    S.op("dve", lambda e: e.tensor_tensor(out=T[2], in0=pb[7][:], in1=sinS, op=ALU.mult), r=[pbk(7), "rot"], w=[("T", 2)])
                    S.op("dve", lambda e, kvh=kvh: e.tensor_tensor(out=SKD[:, l, kvh, 128:640], in0=T[1], in1=T[2], op=ALU.add),
                         r=[("T", 1), ("T", 2)], w=[("SKD", l)])
                for tg in range(4):
                    bank = pbank()
                    proj_tm(sl, slk, 128, 128, tg, bank)
                    copy_op(cpeng(), SV[:, l, 1 + tg, :], pb[bank][:, 0:128], [pbk(bank)], [("SV", l)])
                SCs = [T12.rearrange("p (g k) -> p g k", k=256), carve(0, 4096, F32).rearrange("p (g k) -> p g k", k=256)]
                PBs = [T34b[:, 0:1024].rearrange("p (g k) -> p g k", k=256), carve(4096, 2048, BF16).rearrange("p (g k) -> p g k", k=256)]
                PTss = [T34b[:, 1024:2048].rearrange("p (a q) -> p a q", q=128), carve(6144, 2048, BF16).rearrange("p (a q) -> p a q", q=128)]
                SCK = [[("T", 0), ("T", 1)], [("EQ", 0)]]
                PBK = [[("T", 2)], [("EQ", 1)]]
                PTK = [[("T", 3)], [("EQ", 2)]]
                CTMs = [T[4].bitcast(BF16)[:, 0:512], T[4].bitcast(BF16)[:, 512:1024]]
                ctm_pending = []
                its = [(blk, kvh) for blk in range(4) for kvh in range(2)]

                def swa_stage1(i):
                    blk, kvh = its[i]
                    k = i % 2
                    SC, PB_ = SCs[k], PBs[k]
                    STAT = TTb[:, 32 * k:32 * k + 16]
                    maskc = C_SWAMASK0 if (tt == 0 and blk == 0) else C_SWAMASK

                    def fn(e):
                        ins = None
                        for g in range(4):
                            h = kvh * 4 + g
                            j, hp = h // 2, h % 2
                            ins = e.matmul(pb[4 + g % 2][:, (g // 2) * 256:(g // 2 + 1) * 256],
                                           lhsT=QT[hp * 64:hp * 64 + 64, j, blk * 128:(blk + 1) * 128],
                                           rhs=SKD[hp * 64:hp * 64 + 64, l, kvh, blk * 128:blk * 128 + 256], start=True, stop=True)
                        return ins
                    S.op("pe", fn, r=["QT", ("SKD", l)], w=[pbk(4), pbk(5)])

                def swa_stage1b(i):
                    blk, kvh = its[i]
                    k = i % 2
                    SC, PB_ = SCs[k], PBs[k]
                    STAT = TTb[:, 32 * k:32 * k + 16]
                    maskc = C_SWAMASK0 if (tt == 0 and blk == 0) else C_SWAMASK
                    for hb in range(2):
                        S.op("dve", lambda e, hb=hb: e.scalar_tensor_tensor(
                            out=SC[:, hb::2, :], in0=pb[4 + hb][:].rearrange("p (g k) -> p g k", k=256), scalar=0.125,
                            in1=bass.AP(consts[:].tensor, maskc, [[consts[:].ap[0][0], 128], [0, 2], [1, 256]]),
                            op0=ALU.mult, op1=ALU.add), r=[pbk(4 + hb), "consts"], w=SCK[k])
                    MX, RS, ESn, RINV = STAT[:, 0:4], STAT[:, 4:8], STAT[:, 8:12], STAT[:, 12:16]
                    sk = sinkb[:, l * 8 + kvh * 4:l * 8 + kvh * 4 + 4]
                    S.op("dve", lambda e: e.tensor_reduce(out=MX, in_=SC, axis=AX.X, op=ALU.max), r=SCK[k], w=["TTb"])
                    S.op("dve", lambda e: e.tensor_tensor(out=MX, in0=MX, in1=sk, op=ALU.max), r=["TTb", "sinkb"], w=["TTb"])
                    S.op("dve", lambda e: e.tensor_tensor(out=SC, in0=SC, in1=bass.AP(MX.tensor, MX.offset, [[MX.ap[0][0], 128], [1, 4], [0, 256]]),
                                                          op=ALU.subtract), r=SCK[k] + ["TTb"], w=SCK[k])
                    S.op("act", lambda e: e.activation(out=PB_, in_=SC, func=AF.Exp), r=SCK[k], w=PBK[k])
                    S.op("dve", lambda e: e.tensor_reduce(out=RS, in_=PB_, axis=AX.X, op=ALU.add), r=PBK[k] + ["TTb"], w=["TTb"])
                    S.op("dve", lambda e: e.tensor_tensor(out=ESn, in0=sk, in1=MX, op=ALU.subtract), r=["TTb", "sinkb"], w=["TTb"])
                    S.op("act", lambda e: e.activation(out=ESn, in_=ESn, func=AF.Exp), r=["TTb"], w=["TTb"])
                    S.op("dve", lambda e: e.tensor_tensor(out=RS, in0=RS, in1=ESn, op=ALU.add), r=["TTb"], w=["TTb"])
                    S.op("dve", lambda e: e.reciprocal(out=RINV, in_=RS), r=["TTb"], w=["TTb"])

                def swa_stage2(i):
                    blk, kvh = its[i]
                    k = i % 2
                    PB_, PTs = PBs[k], PTss[k]
                    RINV = TTb[:, 32 * k + 12:32 * k + 16]

                    def fn(e):
                        ins = None
                        pv = pb[3][:].bitcast(BF16)
                        for g in range(4):
                            for kb in range(2):
                                a_ = g * 2 + kb
                                ins = e.transpose(out=pv[:, a_ * 128:(a_ + 1) * 128], in_=PB_[:, g, kb * 128:(kb + 1) * 128], identity=identb[:])
                        return ins
                    S.op("pe", fn, r=PBK[k] + ["identb"], w=[pbk(3)])
                    pv3 = pb[3][:].bitcast(BF16)
                    copy_op("act", PTs.rearrange("p a q -> p (a q)"), pv3[:, 0:1024], [pbk(3)], PTK[k])

                def swa_stage2b(i):
                    blk, kvh = its[i]
                    k = i % 2
                    PB_, PTs = PBs[k], PTss[k]
                    RINV = TTb[:, 32 * k + 12:32 * k + 16]
                    CTM = CTMs[blk % 2]

                    def fn(e):
                        ins = None
                        for g in range(4):
                            e.matmul(pb[6][:, g * 64:(g + 1) * 64], lhsT=PTs[:, g * 2, :], rhs=SV[:, l, blk, kvh * 64:(kvh + 1) * 64],
                                     start=True, stop=False)
                            ins = e.matmul(pb[6][:, g * 64:(g + 1) * 64], lhsT=PTs[:, g * 2 + 1, :], rhs=SV[:, l, blk + 1, kvh * 64:(kvh + 1) * 64],
                                           start=False, stop=True)
                        return ins
                    S.op("pe", fn, r=PTK[k] + [("SV", l)], w=[pbk(6)])
                    S.op("dve", lambda e: e.tensor_tensor(
                        out=CTM[:, kvh * 256:(kvh + 1) * 256].rearrange("p (g d) -> p g d", d=64),
                        in0=pb[6][:, 0:256].rearrange("p (g d) -> p g d", d=64),
                        in1=bass.AP(RINV.tensor, RINV.offset, [[RINV.ap[0][0], 128], [1, 4], [0, 64]]), op=ALU.mult),
                        r=[pbk(6), "TTb"], w=[("T", 4)])
                    if kvh == 1:
                        def ctm_out():
                            def fn(e):
                                ins = None
                                pv = pb[7][:].bitcast(BF16)
                                for j in range(4):
                                    ins = e.transpose(out=pv[:, j * 128:(j + 1) * 128], in_=CTM[:, j * 128:(j + 1) * 128], identity=identb[:])
                                return ins
                            S.op("pe", fn, r=[("T", 4), "identb"], w=[pbk(7)])
                            copy_op("act", mixT[:, 8:12, blk * 128:(blk + 1) * 128],
                                    pb[7][:].bitcast(BF16)[:, 0:512].rearrange("p (j q) -> p j q", q=128), [pbk(7)], [("mixT", 8 + j) for j in range(4)])
                        ctm_pending.append(ctm_out)

                fill = outproj_partial(1, False)
                swa_stage1(0)
                swa_stage1b(0)
                for i in range(8):
                    if i + 1 < 8:
                        swa_stage1(i + 1)
                    swa_stage2(i)
                    while ctm_pending:
                        ctm_pending.pop(0)()
                    if i + 1 < 8:
                        swa_stage1b(i + 1)
                    run_fill(fill, 1)
                    swa_stage2b(i)
                    run_fill(fill, 1)
                while ctm_pending:
                    ctm_pending.pop(0)()
                run_fill(fill, len(fill))
                S.op("pool", lambda e: e.tensor_copy(out=SKD[:, l, :, 0:128], in_=SKD[:, l, :, 512:640]), r=[("SKD", l)], w=[("SKD", l)])
                S.op("pool", lambda e: e.tensor_copy(out=SV[:, l, 0, :], in_=SV[:, l, 4, :]), r=[("SV", l)], w=[("SV", l)])
                if tt == 0:
                    tap("mix_swa_%d" % l, mixT[:, 8:12, :].rearrange("p c t -> p (c t)"), [128, 4 * TT], [("mixT", c) for c in range(8, 12)])

                chk("swa")
                f_sl, f_k = load_slab(win, 4880, 512, KC)
                for h in range(4):
                    bank = pbank()
                    proj_fm(f_sl, f_k, h * 128, 128, bank, hT, ["hT"])
                    lbc = smallc[:, LB0 + l * 4 + h:LB0 + l * 4 + h + 1]
                    omc = smallc[:, OML0 + l * 4 + h:OML0 + l * 4 + h + 1]
                    S.op("act", lambda e, bank=bank: e.activation(out=T[0], in_=pb[bank][:], func=AF.Exp, scale=-1.0), r=[pbk(bank)], w=[("T", 0)])
                    S.op("act", lambda e: e.activation(out=T[0], in_=T[0], func=AF.Identity, bias=1.0), r=[("T", 0)], w=[("T", 0)])
                    S.op("dve", lambda e: e.reciprocal(out=T[0], in_=T[0]), r=[("T", 0)], w=[("T", 0)])
                    S.op("act", lambda e, lbc=lbc, omc=omc: e.activation(out=T[0], in_=T[0], func=AF.Identity, scale=omc, bias=lbc),
                         r=[("T", 0), "smallc"], w=[("T", 0)])
                    S.op("act", lambda e: e.activation(out=T[1], in_=T[0], func=AF.Ln), r=[("T", 0)], w=[("T", 1)])
                    S.op("dve", lambda e: e.tensor_tensor_scan(out=T[2], data0=msk[:], data1=T[1], initial=0.0, op0=ALU.mult, op1=ALU.add),
                         r=[("T", 1), "msk"], w=[("T", 2)])
                    S.op("act", lambda e: e.activation(out=T[0], in_=T[0], func=AF.Identity, scale=-1.0, bias=1.0),
                         r=[("T", 0)], w=[("T", 0)])
                    decay_tables(T[2], ("T", 2), 1.0, h, T[3], T[4], ("T", 3), ("T", 4))
                    S.op("dve", lambda e, h=h: e.tensor_tensor(out=KT[:, h, :], in0=T[0], in1=T[4], op=ALU.mult), r=[("T", 0), ("T", 4)], w=["KT"])
                q_sl, q_k = load_slab(win, 4368, 512, KC)
                for h in range(4):
                    bank = pbank()
                    proj_fm(q_sl, q_k, h * 128, 128, bank, hT, ["hT"])
                    S.op("act", lambda e, bank=bank: e.activation(out=T[0], in_=pb[bank][:], func=AF.Silu), r=[pbk(bank)], w=[("T", 0)])
                    S.op("dve", lambda e, h=h: e.scalar_tensor_tensor(out=QT[:, h, :], in0=T[0], scalar=128.0 ** -0.5, in1=EQ[:, h, :],
                                                                      op0=ALU.mult, op1=ALU.mult), r=[("T", 0)] + EQK, w=["QT"])
                if tt == 0:
                    tap("hg_q_%d" % l, QT.rearrange("p h t -> p (h t)"), [128, 4 * TT], ["QT"])
                    tap("hg_k_%d" % l, KT.rearrange("p h t -> p (h t)"), [128, 4 * TT], ["KT"])
                    tap("hg_ec_%d" % l, EC.rearrange("p k h c -> p (k h c)"), [128, 96], ["EC"])
                    tap("hg_eq_%d" % l, EQ.rearrange("p h t -> p (h t)"), [128, 4 * TT], EQK)
                    tap("smallc_%d" % l, smallc[:], [128, 64], ["smallc"])
                v_and_gate(5392, 5904)
                specs = []
                for h in range(4):
                    vc = (h * 128, h * 128 + 128)
                    ev = tuple(evec_ec(0, 128, kind, h) for kind in range(3))
                    specs.append(((l, h % 2, h % 2, KT[:, h, :], QT[:, h, :], (0, 128), vc, 8 + h, ev, True),
                                  (l, h % 2, QT[:, h, :], (0, 128), vc, vT[:, vb + V_HNW:vb + V_HNW + 1], h, 12 + h)))
                fill = outproj_partial(2, False)
                run_heads(specs, fill)
                fill = outproj_partial(3, True)
                run_fill(fill, len(fill))
                if tt == 0:
                    tap("mix_hg_%d" % l, mixT[:, 12:16, :].rearrange("p c t -> p (c t)"), [128, 4 * TT], [("mixT", c) for c in range(12, 16)])

                chk("hg")
                if tt == 0:
                    tap("x_mid_%d" % l, xT[:].rearrange("p c t -> p (c t)"), [128, KC * TT], [("xT", c) for c in range(KC)])

                chk("outproj")
                norm_to_hT(l, 1, 48, have_ssq=True)
                barrier()
                for s in range(22):
                    g_sl, g_k = load_slab(wffi_d[l], s * 256, 256, KC, half=0)
                    u_sl, u_k = load_slab(wffi_d[l], DFF + s * 256, 256, KC, half=1)
                    for j in range(2):
                        hc = s * 2 + j
                        bg = pbank()
                        proj_fm(g_sl, g_k, j * 128, 128, bg, hT, ["hT"])
                        bu = pbank()
                        proj_fm(u_sl, u_k, j * 128, 128, bu, hT, ["hT"])
                        fs = FS[hc % 2]
                        S.op("act", lambda e, bg=bg, fs=fs: e.activation(out=fs, in_=pb[bg][:], func=AF.Silu), r=[pbk(bg)], w=[("FS", hc % 2)])
                        S.op("dve", lambda e, bu=bu, fs=fs, hc=hc: e.tensor_tensor(out=HID[:, hc, :], in0=fs, in1=pb[bu][:], op=ALU.mult),
                             r=[pbk(bu), ("FS", hc % 2)], w=[("HID", hc)])
                for cg in range(8):
                    banks = [pbank(), pbank()]
                    for kh in range(2):
                        sl, slk = load_slab(wffd_d[l][kh * 2816:(kh + 1) * 2816, :], cg * 256, 256, 22)
                        for j in range(2):
                            def fn(e, sl=sl, j=j, kh=kh, bank=banks[j]):
                                ins = None
                                for kc in range(22):
                                    ins = e.matmul(pb[bank][:, :], lhsT=sl[:, kc, j * 128:(j + 1) * 128], rhs=HID[:, kh * 22 + kc, :],
                                                   start=(kh == 0 and kc == 0), stop=(kh == 1 and kc == 21))
                                return ins
                            S.op("pe", fn, r=list(slk) + [("HID", k) for k in range(kh * 22, kh * 22 + 22)], w=[pbk(banks[j])])
                    for j in range(2):
                        c = cg * 2 + j
                        S.op("dve", lambda e, c=c, bank=banks[j]: e.scalar_tensor_tensor(out=xT[:, c, :], in0=pb[bank][:], scalar=modT[:, l, 80 + c:81 + c],
                                                                                         in1=xT[:, c, :], op0=ALU.mult, op1=ALU.add),
                             r=[pbk(banks[j]), "modT", ("xT", c)], w=[("xT", c)])
                        ssq_chunk(c, "FS", defer=True)
                barrier()
                if tt == 0:
                    tap("x_l0_%d" % l, xT[:].rearrange("p c t -> p (c t)"), [128, KC * TT], [("xT", c) for c in range(KC)])

            for l_ in range(NL):
                layer_body(l_)
            if tt + 1 < NT:
                issue_xload(tt + 1)
            chk("ffn")
            ssq_rstd(have_ssq=True)
            for c in range(KC):
                S.op("dve", lambda e, c=c: e.scalar_tensor_tensor(out=xT[:, c, :], in0=xT[:, c, :], scalar=vT[:, V_FN + c:V_FN + c + 1],
                                                                  in1=RSTD, op0=ALU.mult, op1=ALU.mult),
                     r=[("xT", c), "RSTD", "vT"], w=[("xT", c)])
            for g in range(4):
                for cq in range(4):
                    bank = pbank()

                    def fn(e, g=g, cq=cq, bank=bank):
                        ins = None
                        for j in range(4):
                            c = cq * 4 + j
                            ins = e.transpose(out=pb[bank][:, j * 128:(j + 1) * 128], in_=xT[:, c, g * 128:(g + 1) * 128], identity=identf)
                        return ins
                    S.op("pe", fn, r=[("xT", cq * 4 + j) for j in range(4)] + ["consts"], w=[pbk(bank)])
                    copy_op(cpeng(), stage[:, g % 2, cq * 512:(cq + 1) * 512], pb[bank][:], [pbk(bank)], [("stage", g % 2)])
                S.dma("sp", ringS, out_d[t0 + g * 128:t0 + (g + 1) * 128, :], stage[:, g % 2, :], r=[("stage", g % 2)], w=[("out", tt, g)])
    except _Stop:
        pass
    S.op("sp", None, r=[("out", tt, g) for tt in range(NT) for g in range(4)] + [("tap", n) for n in taps])
    with nc.Block() as block:
        S.emit(block, esems)
    es.close()
    return nc, S


def pack_vecs(b, c, b_ada, norm1_w, norm2_w, gla_gate_b2, hgrn_lb, gla_norm_w, hgrn_norm_w, final_norm_w):
    v = np.zeros((V_ROWS, 128), np.float32)
    for l in range(2):
        base = l * V_LAYER
        v[base + V_BADA:base + V_BADA + 96] = b_ada[l].reshape(96, 128)
        v[base + V_N1:base + V_N1 + 16] = norm1_w[l].reshape(16, 128)
        v[base + V_N2:base + V_N2 + 16] = norm2_w[l].reshape(16, 128)
        v[base + V_B2:base + V_B2 + 2] = gla_gate_b2[l].reshape(2, 128)
        v[base + V_LB:base + V_LB + 4] = hgrn_lb[l].reshape(4, 128)
        v[base + V_GNW] = gla_norm_w[l]
        v[base + V_HNW] = hgrn_norm_w[l]
    v[V_FN:V_FN + 16] = final_norm_w.reshape(16, 128)
    v[V_C:V_C + 16] = c[b].reshape(16, 128)
    return v


_CACHE = {}


def make_in_maps(inputs, ncores=NCORES):
    f = lambda a: np.ascontiguousarray(np.asarray(a))
    x = f(inputs["x"])
    consts = make_consts()
    shared = dict(
        consts=consts, w_ada=f(inputs["w_ada"]), w_in=f(inputs["w_in"]), w2=f(inputs["gla_gate_w2"]),
        sinks=f(inputs["swa_sinks"]).reshape(1, 16), w_out=f(inputs["w_out"]),
        w_ffn_in=f(inputs["w_ffn_in"]), w_ffn_down=f(inputs["w_ffn_down"]))
    maps = []
    for b in range(ncores):
        m = dict(shared)
        m["x"] = x[b]
        m["pos"] = f(inputs["positions"])[b].reshape(1, SEQ).astype(np.int32)
        m["vecs"] = pack_vecs(b, f(inputs["c"]), f(inputs["b_ada"]), f(inputs["norm1_w"]), f(inputs["norm2_w"]),
                              f(inputs["gla_gate_b2"]), f(inputs["hgrn_lb"]), f(inputs["gla_norm_w"]),
                              f(inputs["hgrn_norm_w"]), f(inputs["final_norm_w"]))
        maps.append(m)
    return maps


def kernel(**inputs):
    if "nc" not in _CACHE:
        _CACHE["nc"] = build()[0]
    nc = _CACHE["nc"]
    maps = make_in_maps(inputs)
    res = run_bass_kernel_spmd(nc, maps, core_ids=list(range(NCORES)))
    return np.stack([np.asarray(r["out"]) for r in res.results], axis=0).astype(np.float32)
```
